# Optimizing a Trainium2 kernel written in Bass

```python
import math
import numpy as np
import jax
import jax.numpy as jnp
from jax import lax

D_MODEL = 1024
BATCH = 8
SEQ = 4096
DEPTH = 1

EPS = 1e-6
DN_HEADS = 8
DN_HEAD_DIM = 128
DN_WIDTH = DN_HEADS * DN_HEAD_DIM
CONV_WIDTH = 4
CHUNK = 64
ATT_GROUPS = ((128, 1), (512, 4), (2048, 16))
ATT_HEADS_PER_GROUP = 4
ATT_HEAD_DIM = 128
ATT_N_HEADS = len(ATT_GROUPS) * ATT_HEADS_PER_GROUP
ATT_WIDTH = ATT_N_HEADS * ATT_HEAD_DIM
ATT_OUT_WIDTH = ATT_HEADS_PER_GROUP * ATT_HEAD_DIM
ATT_BLOCK = 128
ROPE_THETA = 500000.0
ROPE_DIM = ATT_HEAD_DIM // 4
D_FF = -(-8 * D_MODEL // (3 * 256)) * 256
IN_SIZES = (DN_WIDTH, DN_WIDTH, DN_WIDTH, DN_WIDTH, DN_HEADS, DN_HEADS,
            ATT_WIDTH, ATT_WIDTH, ATT_WIDTH, D_MODEL, D_MODEL)
IN_DIM = sum(IN_SIZES)

kernel_name = "hybrid_gated_deltanet_dilated_attn_block"


def rms_norm(x, w):
    x32 = x.astype(jnp.float32)
    y = x32 * lax.rsqrt(jnp.mean(x32 * x32, axis=-1, keepdims=True) + EPS)
    return (y * w.astype(jnp.float32)).astype(x.dtype)


def l2_normalize(x):
    x32 = x.astype(jnp.float32)
    return x32 * lax.rsqrt(jnp.sum(x32 * x32, axis=-1, keepdims=True) + EPS)


def causal_depthwise_conv(x, w):
    k_width = w.shape[0]
    seq = x.shape[1]
    xp = jnp.pad(x, ((0, 0), (k_width - 1, 0), (0, 0)))
    return sum(xp[:, j:j + seq] * w[j] for j in range(k_width))


def _to_chunks(t, n_chunks):
    b, _, h = t.shape[:3]
    t = t.reshape((b, n_chunks, CHUNK, h) + t.shape[3:])
    return jnp.swapaxes(jnp.swapaxes(t, 0, 1), 2, 3)


def gated_delta_rule(q, k, v, g, beta):
    f32 = jnp.float32
    b, seq, h, dk = q.shape
    dv = v.shape[-1]
    n = seq // CHUNK
    q, k, v = (_to_chunks(t.astype(f32), n) for t in (q, k, v))
    g, beta = (_to_chunks(t.astype(f32), n) for t in (g, beta))
    gc = jnp.cumsum(g, axis=-1)
    idx = jnp.arange(CHUNK)
    causal = idx[:, None] >= idx[None, :]
    strict = idx[:, None] > idx[None, :]
    decay = jnp.exp(jnp.where(causal, gc[..., :, None] - gc[..., None, :], -jnp.inf))
    kb = k * beta[..., None]
    vb = v * beta[..., None]
    a = jnp.where(strict, jnp.einsum('nbhcd,nbhsd->nbhcs', kb, k) * decay, 0.0)
    eye = jnp.eye(CHUNK, dtype=f32)
    t_inv = lax.linalg.triangular_solve(a + eye, jnp.broadcast_to(eye, a.shape),
                                        left_side=True, lower=True, unit_diagonal=True)
    u = jnp.einsum('nbhcs,nbhse->nbhce', t_inv, vb)
    w = jnp.einsum('nbhcs,nbhsd->nbhcd', t_inv, kb * jnp.exp(gc)[..., None])
    qk = jnp.einsum('nbhcd,nbhsd->nbhcs', q, k) * decay
    q_dec = q * jnp.exp(gc)[..., None]
    k_dec = k * jnp.exp(gc[..., -1:] - gc)[..., None]
    g_last = jnp.exp(gc[..., -1])

    def step(state, xs):
        u_n, w_n, qk_n, qd_n, kd_n, gl_n = xs
        v_new = u_n - jnp.einsum('bhcd,bhde->bhce', w_n, state)
        o_n = jnp.einsum('bhcd,bhde->bhce', qd_n, state) + jnp.einsum('bhcs,bhse->bhce', qk_n, v_new)
        state = state * gl_n[..., None, None] + jnp.einsum('bhcd,bhce->bhde', kd_n, v_new)
        return state, o_n

    s0 = jnp.zeros((b, h, dk, dv), f32)
    _, o = lax.scan(step, s0, (u, w, qk, q_dec, k_dec, g_last))
    o = jnp.swapaxes(jnp.swapaxes(o, 2, 3), 0, 1)
    return o.reshape(b, seq, h, dv)


def partial_rope(x, positions):
    half = ROPE_DIM // 2
    inv_freq = jnp.power(ROPE_THETA, -jnp.arange(half, dtype=jnp.float32) * (2.0 / ROPE_DIM))
    ang = positions.astype(jnp.float32)[:, None] * inv_freq[None, :]
    cos = jnp.cos(ang)[None, :, None, :]
    sin = jnp.sin(ang)[None, :, None, :]
    x1 = x[..., :half].astype(jnp.float32)
    x2 = x[..., half:ROPE_DIM].astype(jnp.float32)
    rest = x[..., ROPE_DIM:].astype(jnp.float32)
    return jnp.concatenate([x1 * cos - x2 * sin, x2 * cos + x1 * sin, rest], axis=-1).astype(x.dtype)


def dilated_window_attention(q, k, v, window, dilation):
    b, seq, h, d = q.shape
    n_keys = window // dilation
    span = dilation * ATT_BLOCK
    lp = -(-seq // span) * span
    m_len = lp // dilation
    nb = m_len // ATT_BLOCK

    def to_blocks(t):
        t = jnp.pad(t, ((0, 0), (0, lp - seq), (0, 0), (0, 0)))
        t = jnp.swapaxes(t.reshape(b, m_len, dilation, h, d), 1, 2)
        return t.reshape(b, dilation, nb, ATT_BLOCK, h, d)

    def with_prev(t):
        prev = jnp.pad(t, ((0, 0), (0, 0), (1, 0), (0, 0), (0, 0), (0, 0)))[:, :, :-1]
        return jnp.concatenate([prev, t], axis=3)

    def from_blocks(t):
        t = t.reshape((b, dilation, m_len) + t.shape[4:])
        t = jnp.swapaxes(t, 1, 2)
        return t.reshape((b, lp) + t.shape[3:])[:, :seq]

    qb = to_blocks(q)
    kc = with_prev(to_blocks(k))
    vc = with_prev(to_blocks(v))
    s = jnp.einsum('brnqhd,brnkhd->brnhqk', qb, kc, preferred_element_type=jnp.float32) * (d ** -0.5)
    qi = jnp.arange(ATT_BLOCK)[:, None]
    kj = jnp.arange(2 * ATT_BLOCK)[None, :]
    dist = qi + ATT_BLOCK - kj
    blk = jnp.arange(nb)[:, None, None]
    valid = (dist >= 0) & (dist <= n_keys) & ((blk > 0) | (kj >= ATT_BLOCK))
    s = jnp.where(valid[:, None], s, -jnp.inf)
    m = jnp.max(s, axis=-1, keepdims=True)
    p = jnp.exp(s - m)
    den = jnp.sum(p, axis=-1, keepdims=True)
    o = jnp.einsum('brnhqk,brnkhd->brnqhd', p / den, vc.astype(jnp.float32))
    lse = jnp.swapaxes((m + jnp.log(den))[..., 0], 3, 4)
    return from_blocks(o), from_blocks(lse)


def setup_inputs(seed: int = 0) -> dict:
    key = jax.random.key(seed)
    ks = jax.random.split(key, 16)
    f32 = jnp.float32

    def dense(k, shape, fan_in):
        return jax.random.normal(k, shape, f32) * (fan_in ** -0.5)

    def gain(k, shape):
        return 1.0 + 0.02 * jax.random.normal(k, shape, f32)

    x = jax.random.normal(ks[0], (BATCH, SEQ, D_MODEL), f32)
    norm1_w = gain(ks[1], (DEPTH, D_MODEL))
    w_in = dense(ks[2], (DEPTH, D_MODEL, IN_DIM), D_MODEL)
    conv_w = dense(ks[3], (DEPTH, CONV_WIDTH, 3 * DN_WIDTH), CONV_WIDTH)
    a_log = jnp.log(jax.random.uniform(ks[4], (DEPTH, DN_HEADS), f32, minval=1.0, maxval=16.0))
    dt = jnp.exp(jax.random.uniform(ks[5], (DEPTH, DN_HEADS), f32,
                                    minval=math.log(1e-3), maxval=math.log(1e-1)))
    dt_bias = dt + jnp.log(-jnp.expm1(-dt))
    dn_norm_w = gain(ks[6], (DEPTH, DN_HEAD_DIM))
    w_proj_a = dense(ks[7], (DEPTH, DN_WIDTH, D_MODEL), DN_WIDTH)
    w_proj_b = dense(ks[8], (DEPTH, ATT_OUT_WIDTH, D_MODEL), ATT_OUT_WIDTH)
    w_out = dense(ks[9], (DEPTH, D_MODEL, D_MODEL), D_MODEL)
    norm2_w = gain(ks[10], (DEPTH, D_MODEL))
    w_gate_up = dense(ks[11], (DEPTH, D_MODEL, 2 * D_FF), D_MODEL)
    w_down = dense(ks[12], (DEPTH, D_FF, D_MODEL), D_FF)
    final_norm_w = gain(ks[13], (D_MODEL,))
    return {"x": x, "norm1_w": norm1_w, "w_in": w_in, "conv_w": conv_w, "a_log": a_log,
            "dt_bias": dt_bias, "dn_norm_w": dn_norm_w, "w_proj_a": w_proj_a, "w_proj_b": w_proj_b,
            "w_out": w_out, "norm2_w": norm2_w, "w_gate_up": w_gate_up, "w_down": w_down,
            "final_norm_w": final_norm_w}


def reference(x, norm1_w, w_in, conv_w, a_log, dt_bias, dn_norm_w, w_proj_a, w_proj_b,
              w_out, norm2_w, w_gate_up, w_down, final_norm_w):
    f32 = jnp.float32
    b, seq, _ = x.shape
    positions = jnp.arange(seq)
    splits = np.cumsum(IN_SIZES)[:-1].tolist()
    for i in range(DEPTH):
        h = rms_norm(x, norm1_w[i])
        proj = h @ w_in[i]
        dq, dk, dv, dz, db, da, aq, ak, av, ga, gb = jnp.split(proj, splits, axis=-1)

        qkv = jax.nn.silu(causal_depthwise_conv(jnp.concatenate([dq, dk, dv], axis=-1), conv_w[i]))
        cq, ck, cv = jnp.split(qkv, 3, axis=-1)
        q = l2_normalize(cq.reshape(b, seq, DN_HEADS, DN_HEAD_DIM)) * (DN_HEAD_DIM ** -0.5)
        k = l2_normalize(ck.reshape(b, seq, DN_HEADS, DN_HEAD_DIM))
        v = cv.reshape(b, seq, DN_HEADS, DN_HEAD_DIM)
        beta = jax.nn.sigmoid(db.astype(f32))
        g = -jnp.exp(a_log[i].astype(f32)) * jax.nn.softplus(da.astype(f32) + dt_bias[i].astype(f32))
        o_a = gated_delta_rule(q, k, v, g, beta)
        o_a = rms_norm(o_a, dn_norm_w[i]) * jax.nn.silu(dz.reshape(b, seq, DN_HEADS, DN_HEAD_DIM).astype(f32))
        y_a = o_a.reshape(b, seq, DN_WIDTH).astype(x.dtype) @ w_proj_a[i]

        qr = partial_rope(aq.reshape(b, seq, ATT_N_HEADS, ATT_HEAD_DIM), positions)
        kr = partial_rope(ak.reshape(b, seq, ATT_N_HEADS, ATT_HEAD_DIM), positions)
        vv = av.reshape(b, seq, ATT_N_HEADS, ATT_HEAD_DIM)
        outs, lses = [], []
        for gi, (window, dil) in enumerate(ATT_GROUPS):
            sl = slice(gi * ATT_HEADS_PER_GROUP, (gi + 1) * ATT_HEADS_PER_GROUP)
            o_g, lse_g = dilated_window_attention(qr[:, :, sl], kr[:, :, sl], vv[:, :, sl], window, dil)
            outs.append(o_g)
            lses.append(lse_g)
        alpha = jax.nn.softmax(jnp.stack(lses), axis=0)
        o_b = jnp.sum(alpha[..., None] * jnp.stack(outs), axis=0)
        y_b = o_b.reshape(b, seq, ATT_OUT_WIDTH).astype(x.dtype) @ w_proj_b[i]

        merged = jax.nn.sigmoid(ga) * y_a + jax.nn.sigmoid(gb) * y_b
        x = x + merged @ w_out[i]

        h2 = rms_norm(x, norm2_w[i])
        gate, up = jnp.split(h2 @ w_gate_up[i], 2, axis=-1)
        x = x + (jax.nn.silu(gate) * up) @ w_down[i]
    return rms_norm(x, final_norm_w)
```

```python
import contextlib
import numpy as np
import concourse.bass as bass
import concourse.mybir as mybir
from concourse.bass_utils import run_bass_kernel_spmd

F32 = mybir.dt.float32
BF16 = mybir.dt.bfloat16
ALU = mybir.AluOpType
AF = mybir.ActivationFunctionType

T = 4096
D = 1024
IN_DIM = 10768
DFF = 2816
EPS = 1e-6
OFF = dict(dq=0, dk=1024, dv=2048, dz=3072, db=4096, da=4104, aq=4112, ak=5648, av=7184,
           ga=8720, gb=9744)
GROUPS = ((128, 1), (512, 4), (2048, 16))
CH = 128


class Ins:
    __slots__ = ("eng", "fn", "deps", "odeps", "dma", "need", "sem", "val", "cost", "seg", "idx",
                 "prio", "nrem", "succ", "fin", "waits")

    def __init__(self, eng, fn, dma):
        self.eng = eng
        self.fn = fn
        self.deps = []
        self.odeps = []
        self.dma = dma
        self.need = dma
        self.sem = None
        self.val = 0
        self.cost = 0.5
        self.seg = 0


EMBED_DMA = False


class Prog:
    ENGS = ("pe", "act", "dve", "pool", "sp")

    def __init__(self, nc, n_dma_sems=42):
        self.nc = nc
        self.lists = {e: [] for e in self.ENGS}
        self.bufs = {}
        self.excl = {}
        self.n_dma_sems = n_dma_sems
        self.order = []
        self.seg = 0

    def emit(self, eng, fn, reads=(), writes=(), excl=(), dma=False, cost=0.5):
        ins = Ins(eng, fn, dma)
        ins.cost = cost
        ins.seg = self.seg
        deps = {}
        for k in reads:
            st = self.bufs.get(k)
            if st is None:
                st = self.bufs[k] = [None, []]
            if st[0] is not None:
                deps[id(st[0])] = st[0]
        for k in writes:
            st = self.bufs.get(k)
            if st is None:
                st = self.bufs[k] = [None, []]
            if st[0] is not None:
                deps[id(st[0])] = st[0]
            for r in st[1]:
                deps[id(r)] = r
        odeps_extra = []
        for k in excl:
            st = self.excl.get(k)
            if st is None:
                st = self.excl[k] = {}
            for e2, i2 in st.items():
                if e2 != eng:
                    deps[id(i2)] = i2
                else:
                    odeps_extra.append(i2)
            st[eng] = ins
        for k in reads:
            self.bufs[k][1].append(ins)
        for k in writes:
            self.bufs[k] = [ins, []]
        dl = []
        for d in deps.values():
            if d is ins:
                continue
            if d.eng == "pe" and eng == "pe" and not d.dma and not dma:
                continue
            dl.append(d)
        ins.deps = dl
        ins.odeps = [d for d in deps.values() if d is not ins] + [d for d in odeps_extra if d is not ins]
        self.lists[eng].append(ins)
        self.order.append(ins)
        return ins

    def barrier(self):
        self.seg += 1
        self.bufs = {}
        self.excl = {}

    def schedule(self):
        LAT = 0.45
        segs = {}
        for ins in self.order:
            segs.setdefault(ins.seg, []).append(ins)
        new_order = []
        prev_lasts = []
        for sg in sorted(segs):
            L = segs[sg]
            inseg = set(id(i) for i in L)
            for i in L:
                i.succ = []
                i.nrem = 0
            for i in L:
                seen_ = set()
                for d in i.odeps:
                    if id(d) in inseg and id(d) not in seen_:
                        seen_.add(id(d))
                        d.succ.append(i)
                        i.nrem += 1
            for i in reversed(L):
                m = 0.0
                for s_ in i.succ:
                    if s_.prio > m:
                        m = s_.prio
                i.prio = m + i.cost + LAT
            ready = {e: [] for e in self.ENGS}
            for i in L:
                i.fin = 0.0
                if i.nrem == 0:
                    ready[i.eng].append(i)
            rdy_t = {}
            tfree = {e: 0.0 for e in self.ENGS}
            out = []
            nleft = len(L)
            while nleft:
                best = None
                for e in self.ENGS:
                    rl = ready[e]
                    if not rl:
                        continue
                    bi = None
                    for c in rl:
                        st_ = rdy_t.get(id(c), 0.0)
                        if st_ < tfree[e]:
                            st_ = tfree[e]
                        key = (st_, -c.prio)
                        if bi is None or key < bi[0]:
                            bi = (key, c)
                    if best is None or bi[0] < best[0]:
                        best = bi
                (st_, _), c = best
                ready[c.eng].remove(c)
                if c.dma:
                    tfree[c.eng] = st_ + (1.0 if c.eng == "pool" else 0.1)
                    c.fin = st_ + c.cost
                else:
                    tfree[c.eng] = st_ + c.cost
                    c.fin = st_ + c.cost
                out.append(c)
                nleft -= 1
                for s_ in c.succ:
                    t_ = c.fin + (0.0 if (s_.eng == c.eng and not c.dma) else LAT)
                    if rdy_t.get(id(s_), 0.0) < t_:
                        rdy_t[id(s_)] = t_
                    s_.nrem -= 1
                    if s_.nrem == 0:
                        ready[s_.eng].append(s_)
            if prev_lasts:
                firsts = {}
                for i in out:
                    if i.eng not in firsts:
                        firsts[i.eng] = i
                for i in firsts.values():
                    i.deps = list(i.deps) + [d for d in prev_lasts if d is not i]
            lasts = []
            for e in self.ENGS:
                for i in reversed(out):
                    if i.eng == e and not i.dma:
                        lasts.append(i)
                        break
            dm = [i for i in out if i.dma]
            lasts.extend([i for i in dm if i.eng == "pool"][-self.n_dma_sems:])
            lasts.extend([i for i in dm if i.eng != "pool"][-self.n_dma_sems:])
            if not lasts:
                lasts = prev_lasts
            prev_lasts = lasts
            new_order.extend(out)
        self.order = new_order
        self.lists = {e: [i for i in new_order if i.eng == e] for e in self.ENGS}

    def finalize(self, stack):
        nc = self.nc
        self.schedule()
        for e in self.ENGS:
            for n_, ins in enumerate(self.lists[e]):
                ins.idx = n_
        for ins in self.order:
            best = {}
            keep = []
            for d in ins.deps:
                if d.dma:
                    keep.append(d)
                elif d.eng not in best or best[d.eng].idx < d.idx:
                    best[d.eng] = d
            ins.deps = keep + list(best.values())
            for d in ins.deps:
                d.need = True
        esem = {e: stack.enter_context(nc.semaphore("s_" + e)) for e in self.ENGS}
        dsems = [stack.enter_context(nc.semaphore("d%d" % i)) for i in range(self.n_dma_sems)]
        cnt = {e: 0 for e in self.ENGS}
        dcnt = [0] * self.n_dma_sems
        dlast = [None] * self.n_dma_sems
        n_sw = self.n_dma_sems // 3
        n_hw = self.n_dma_sems - n_sw
        nd_hw = 0
        nd_sw = 0
        for ins in self.order:
            if ins.dma:
                if ins.eng == "pool":
                    s = n_hw + (nd_sw % n_sw)
                    nd_sw += 1
                else:
                    s = nd_hw % n_hw
                    nd_hw += 1
                if dlast[s] is not None:
                    ins.deps.append(dlast[s])
                dcnt[s] += 16
                ins.sem = dsems[s]
                ins.val = dcnt[s]
                dlast[s] = ins
            elif ins.need:
                cnt[ins.eng] += 1
                ins.sem = esem[ins.eng]
                ins.val = cnt[ins.eng]
        self.counts = dict(cnt)
        fin = [d for d in dlast if d is not None]
        lists = self.lists
        nwaits = {e: 0 for e in self.ENGS}
        known = {e: {} for e in self.ENGS}
        vc = {}
        for ins in self.order:
            kn = known[ins.eng]
            cand = {}
            for d in ins.deps:
                key = id(d.sem)
                if kn.get(key, 0) < d.val and (key not in cand or cand[key].val < d.val):
                    cand[key] = d
            wl = sorted(cand.values(), key=lambda d: -getattr(d, "fin", 0.0))
            kept = []
            for d in wl:
                key = id(d.sem)
                if kn.get(key, 0) >= d.val:
                    continue
                kept.append((d.sem, d.val))
                kn[key] = d.val
                for k2, v2 in vc[id(d)].items():
                    if kn.get(k2, 0) < v2:
                        kn[k2] = v2
            ins.waits = kept
            if ins.sem is not None:
                snap = dict(kn)
                snap[id(ins.sem)] = ins.val
                vc[id(ins)] = snap
        self.known_final = known

        def run(engname, eng):
            for ins in lists[engname]:
                wl_ = list(ins.waits)
                emb = None
                if wl_ and (EMBED_DMA or not ins.dma):
                    emb = wl_.pop(0)
                for (sm_, vl_) in reversed(wl_):
                    eng.wait_ge(sm_, vl_)
                    nwaits[engname] += 1
                r = ins.fn(eng)
                if emb is not None:
                    r._wait_ge(emb[0], emb[1])
                if ins.sem is not None:
                    r.then_inc(ins.sem, 16 if ins.dma else 1)
            if engname == "sp":
                kn = known["sp"]
                for d in fin:
                    if kn.get(id(d.sem), 0) < d.val:
                        eng.wait_ge(d.sem, d.val)

        with nc.Block() as block:
            @block.tensor
            def _(e):
                run("pe", e)

            @block.scalar
            def _(e):
                run("act", e)

            @block.vector
            def _(e):
                run("dve", e)

            @block.gpsimd
            def _(e):
                run("pool", e)

            @block.sync
            def _(e):
                run("sp", e)
        self.nwaits = nwaits


class Arena:
    def __init__(self, t, words):
        self.t = t
        self.words = words
        self.off = 0

    def reset(self):
        self.off = 0

    def alloc(self, parts, shape, dt):
        n = 1
        for s in shape:
            n *= s
        nbytes = n * (2 if dt == BF16 else 4)
        w = (nbytes + 3) // 4
        w = (w + 15) // 16 * 16
        assert self.off + w <= self.words, ("arena overflow", self.off, w, self.words)
        v = self.t[:, self.off:self.off + w]
        self.off += w
        if dt == BF16:
            v = v.bitcast(BF16)
        v = v[0:parts, 0:n]
        if len(shape) == 2:
            v = v.rearrange("p (a b) -> p a b", a=shape[0])
        elif len(shape) == 3:
            v = v.rearrange("p (a b c) -> p a b c", a=shape[0], b=shape[1])
        return v


def bc_inner(v, n):
    return bass.AP(v.tensor, v.offset, [list(d) for d in v.ap] + [[0, n]])


def bc_mid(v, n):
    a = [list(d) for d in v.ap]
    return bass.AP(v.tensor, v.offset, [a[0], [0, n]] + a[1:])


def _fsz(ap):
    n = 1
    for d in ap.shape[1:]:
        n *= d
    return n


def MM(P, out, lhsT, rhs, start=True, stop=True, reads=(), excl=()):
    n = _fsz(rhs)
    c = max(0.056, n / 2400.0 + 0.004) * (3.0 if rhs.dtype == F32 else 1.0)
    P.emit("pe", lambda e: e.matmul(out, lhsT=lhsT, rhs=rhs, start=start, stop=stop), reads=reads, excl=excl, cost=c)


def TRN(P, out, in_, ident, reads=(), excl=()):
    P.emit("pe", lambda e: e.transpose(out=out, in_=in_, identity=ident), reads=reads, excl=excl, cost=0.07)


def ACTV(P, out, in_, func, reads=(), writes=(), excl=(), **kw):
    P.emit("act", lambda e: e.activation(out=out, in_=in_, func=func, **kw), reads=reads, writes=writes, excl=excl,
           cost=0.25 + _fsz(out) / 1200.0)


def CPY(P, eng, out, in_, reads=(), writes=(), excl=()):
    if eng == "act":
        P.emit("act", lambda e: e.copy(out=out, in_=in_), reads=reads, writes=writes, excl=excl, cost=0.25 + _fsz(out) / 1200.0)
    else:
        P.emit(eng, lambda e: e.tensor_copy(out=out, in_=in_), reads=reads, writes=writes, excl=excl, cost=_vc(eng, out))


def _vc(eng, out):
    n = _fsz(out)
    if eng == "pool":
        return 0.15 + n / 480.0
    return 0.16 + n / 960.0


def TT(P, eng, out, in0, in1, op, reads=(), writes=(), excl=()):
    P.emit(eng, lambda e: e.tensor_tensor(out=out, in0=in0, in1=in1, op=op), reads=reads, writes=writes, excl=excl, cost=_vc(eng, out))


def TS(P, eng, out, in0, s1, s2, op0, op1=None, reads=(), writes=(), excl=()):
    if op1 is None:
        P.emit(eng, lambda e: e.tensor_scalar(out=out, in0=in0, scalar1=s1, scalar2=None, op0=op0), reads=reads, writes=writes, excl=excl, cost=_vc(eng, out))
    else:
        P.emit(eng, lambda e: e.tensor_scalar(out=out, in0=in0, scalar1=s1, scalar2=s2, op0=op0, op1=op1), reads=reads, writes=writes, excl=excl, cost=_vc(eng, out))


def STT(P, eng, out, in0, scalar, in1, op0, op1, reads=(), writes=(), excl=()):
    P.emit(eng, lambda e: e.scalar_tensor_tensor(out=out, in0=in0, scalar=scalar, in1=in1, op0=op0, op1=op1),
           reads=reads, writes=writes, excl=excl, cost=_vc(eng, out))


def RCP(P, out, in_, reads=(), writes=()):
    P.emit("dve", lambda e: e.reciprocal(out=out, in_=in_), reads=reads, writes=writes, cost=_vc("dve", out))


def DMA(P, eng, out, in_, reads=(), writes=()):
    nb = 4 * 128 * _fsz(out if len(out.shape) >= len(in_.shape) else in_)
    P.emit(eng, lambda e: e.dma_start(out=out, in_=in_), reads=reads, writes=writes, dma=True, cost=4.0 + nb / 80e3)


def MSET(P, ap, val, writes=()):
    P.emit("pool", lambda e: e.memset(ap, val), writes=writes, cost=0.15 + _fsz(ap) / 960.0)


class K:
    pass


def build_program(debug=False, stop_after=99):
    nc = bass.Bass("TRN2", target_bir_lowering=False)
    k = K()
    k.nc = nc
    k.debug = debug

    def din(name, shape):
        return nc.dram_tensor(name, list(shape), F32, kind="ExternalInput").ap()

    k.x = din("x", [T, D])
    k.norm1_w = din("norm1_w", [1, D])
    k.w_in = din("w_in", [D, IN_DIM])
    k.cwT = din("cwT", [128, 96])
    k.a_log = din("a_log", [1, 8])
    k.dt_bias = din("dt_bias", [1, 8])
    k.dn_norm_w = din("dn_norm_w", [128, 1])
    k.w_proj_a = din("w_proj_a", [D, D])
    k.w_proj_b = din("w_proj_b", [512, D])
    k.w_out = din("w_out", [D, D])
    k.norm2_w = din("norm2_w", [1, D])
    k.w_gate_up = din("w_gate_up", [D, 2 * DFF])
    k.w_down = din("w_down", [DFF, D])
    k.final_norm_w = din("final_norm_w", [1, D])
    k.c_ident = din("c_ident", [128, 128])
    k.c_maskb = din("c_maskb", [128, 256])
    k.c_cos = din("c_cos", [32, T])
    k.c_sin = din("c_sin", [32, T])
    k.c_pm = din("c_pm", [32, 32])
    k.c_tri = din("c_tri", [CH, CH])
    k.c_u = din("c_u", [CH, CH])
    k.out = nc.dram_tensor("out", [T, D], F32, kind="ExternalOutput").ap()
    k.oaT_d = nc.dram_tensor("oaT_d", [8, 128, T], BF16, kind="Internal").ap()
    k.obT_d = nc.dram_tensor("obT_d", [4, 128, T], BF16, kind="Internal").ap()
    k.x1_d = nc.dram_tensor("x1_d", [T, D], F32, kind="Internal").ap()
    if debug:
        k.dbg_hT = nc.dram_tensor("dbg_hT", [128, 8, T], BF16, kind="ExternalOutput").ap()
        k.dbg_oa = nc.dram_tensor("dbg_oa", [8, 128, T], BF16, kind="ExternalOutput").ap()
        k.dbg_ob = nc.dram_tensor("dbg_ob", [4, 128, T], BF16, kind="ExternalOutput").ap()
        k.dbg_x1 = nc.dram_tensor("dbg_x1", [T, D], F32, kind="ExternalOutput").ap()

    with contextlib.ExitStack() as st:
        P = Prog(nc)
        k.P = P
        sb = lambda name, shape, dt: st.enter_context(nc.sbuf_tensor(name, shape, dt))
        ARW = 50 * 1024
        k.arena_t = sb("arena", [128, ARW], F32)
        k.A = Arena(k.arena_t, ARW)
        k.hT = k.A.alloc(128, (8, T), BF16)
        k.hT_end = k.A.off
        k.ident_f = sb("ident_f", [128, 128], F32)
        k.ident_b = sb("ident_b", [128, 128], BF16)
        k.ones_b = sb("ones_b", [128, 128], BF16)
        k.ones_f = sb("ones_f", [128, 128], F32)
        k.nw_bc = sb("nw_bc", [128, D], F32)
        k.stat = sb("stat", [128, 64], F32)
        k.stat2 = sb("stat2", [128, 64], F32)
        k.psf = [st.enter_context(nc.psum_tensor("psf%d" % i, [128, 512], F32)) for i in range(8)]
        k.psb = [k.psf[6][:, :].bitcast(BF16), k.psf[7][:, :].bitcast(BF16)]

        DMA(P, "sp", k.ident_f[:], k.c_ident, writes=["ident_f"])
        DMA(P, "pool", k.ident_b[:], k.c_ident, writes=["ident_b"])
        MSET(P, k.ones_b[:], 1.0, writes=["ones_b"])
        MSET(P, k.ones_f[:], 1.0, writes=["ones_f"])

        hTk = ["hT%d" % i for i in range(32)]
        phase1(k, k.x, k.norm1_w, lambda tt: k.hT[:, :, tt * 128:(tt + 1) * 128], hTk, 32, 0)
        if debug:
            DMA(P, "sp", k.dbg_hT, k.hT, reads=hTk)
        if stop_after >= 2:
            P.barrier()
            k.A.off = k.hT_end
            phase2a(k)
        if stop_after >= 3:
            P.barrier()
            k.A.off = k.hT_end
            phase2b(k)
        if stop_after >= 4:
            P.barrier()
            k.A.off = k.hT_end
            phase3a(k)
        if stop_after >= 5:
            P.barrier()
            k.A.off = 0
            phase3b(k)
        P.finalize(st)
        k.counts = (P.counts, P.nwaits, {e: len(v) for e, v in P.lists.items()})
    return nc, k


def phase1(k, src, nw, dst_fn, dst_keys, ntiles, tile0, tag="p1", bufs=None):
    P, A = k.P, k.A
    NB = 12
    if bufs is None:
        bufs = ([A.alloc(128, (D,), F32) for _ in range(NB)], [A.alloc(128, (D,), BF16) for _ in range(NB)],
                A.alloc(128, (D,), BF16))
    xt, xn, junk = bufs
    nwk = tag + "nw"
    DMA(P, "sp", k.nw_bc[:], bass.AP(nw.tensor, nw.offset, [[0, 128], [1, D]]), writes=[nwk])
    st = k.stat
    for i in range(ntiles):
        s = i % NB
        pbi = i % 2
        tt = tile0 + i
        xk, nk = "%sxt%d" % (tag, s), "%sxn%d" % (tag, s)
        sk = "%sst%d" % (tag, s)
        DMA(P, "sp", xt[s], src[tt * 128:(tt + 1) * 128, :], writes=[xk])
        ACTV(P, junk, xt[s], AF.Square, reads=[xk], writes=[tag + "junk", sk], accum_out=st[:, s:s + 1])
        ACTV(P, st[:, 16 + s:17 + s], st[:, s:s + 1], AF.Sqrt, reads=[sk], writes=[sk + "b"], scale=1.0 / D, bias=EPS)
        RCP(P, st[:, 32 + s:33 + s], st[:, 16 + s:17 + s], reads=[sk + "b"], writes=[sk + "c"])
        STT(P, "dve", xn[s], xt[s], st[:, 32 + s:33 + s], k.nw_bc[:], ALU.mult, ALU.mult, reads=[xk, sk + "c", nwk], writes=[nk])
        pb = k.psb[pbi]
        for kc in range(8):
            TRN(P, pb[:, kc * 128:(kc + 1) * 128], xn[s][:, kc * 128:(kc + 1) * 128], k.ident_b[:],
                reads=[nk, "ident_b"], excl=["psb%d" % pbi])
        CPY(P, "dve" if i % 2 else "act", dst_fn(i), pb[:].rearrange("p (a b) -> p a b", a=8), writes=[dst_keys[i]], excl=["psb%d" % pbi])
    return bufs


def wload(k, dst, src_rows, col0, ncols, nkc, key, row0=0):
    src = src_rows[row0:row0 + nkc * 128, col0:col0 + ncols].rearrange("(kc p) c -> p kc c", p=128)
    DMA(k.P, "pool", dst, src, writes=[key])


def f2(v):
    return v.rearrange("p a b -> p (a b)")


def run_streams(gens, lead=0):
    active = list(gens)
    for _ in range(lead):
        try:
            next(active[0])
        except StopIteration:
            active.pop(0)
            break
    while active:
        for g_ in list(active):
            try:
                next(g_)
            except StopIteration:
                active.remove(g_)


def phase2a(k):
    P, A = k.P, k.A
    psf = k.psf
    hTk = ["hT%d" % i for i in range(32)]
    hT = k.hT

    def hkeys(t0, n):
        return hTk[t0 // 128:(t0 + n + 127) // 128]

    C = CH
    NCB = 512 // C
    NCH = T // C
    NLV = 6 if C == 128 else 5
    tri = A.alloc(C, (C,), F32)
    um = A.alloc(C, (C,), F32)
    cw = A.alloc(128, (96,), F32)
    dnwh = A.alloc(128, (1,), F32)
    alog = A.alloc(C, (8,), F32)
    dtb = A.alloc(C, (8,), F32)
    nA = A.alloc(C, (8,), F32)
    mhalf = A.alloc(128, (512,), F32)
    DMA(P, "sp", tri, k.c_tri, writes=["tri"])
    DMA(P, "sp", um, k.c_u, writes=["um"])
    DMA(P, "sp", cw, k.cwT, writes=["cw"])
    DMA(P, "sp", dnwh, k.dn_norm_w, writes=["dnw0"])
    TS(P, "dve", dnwh, dnwh, 1.0, None, ALU.mult, reads=["dnw0"], writes=["dnwh"])
    DMA(P, "sp", alog, bass.AP(k.a_log.tensor, 0, [[0, C], [1, 8]]), writes=["alog"])
    DMA(P, "sp", dtb, bass.AP(k.dt_bias.tensor, 0, [[0, C], [1, 8]]), writes=["dtb"])
    MSET(P, mhalf, -0.5, writes=["mhalf"])
    ACTV(P, nA, alog, AF.Exp, reads=["alog"], writes=["nA0"])
    TS(P, "dve", nA, nA, -1.0, None, ALU.mult, reads=["nA0"], writes=["nA"])
    hbeta = A.alloc(C, (8, NCH), F32)
    nbeta = A.alloc(C, (8, NCH), F32)
    g = A.alloc(C, (8, NCH), F32)
    egc = A.alloc(C, (8, NCH), F32)
    edl = A.alloc(C, (8, NCH), F32)
    begc = A.alloc(C, (8, NCH), F32)
    egl = A.alloc(128, (8, NCH), F32)
    shared_end = A.off
    wbd = A.alloc(128, (8, 16), BF16)
    bd = A.alloc(C, (NCH, 16), F32)
    beta = A.alloc(C, (8, NCH), F32)
    gc = A.alloc(C, (8, NCH), F32)
    glast = A.alloc(128, (8, NCH), F32)
    tmpg = A.alloc(C, (8, NCH), F32)
    wload(k, wbd, k.w_in, OFF["db"], 16, 8, "wbd")
    for n in range(NCH):
        bank = n // 32
        for kc in range(8):
            MM(P, psf[bank][0:C, (n % 32) * 16:(n % 32) * 16 + 16], hT[:, kc, n * C:(n + 1) * C], wbd[:, kc, :],
               start=(kc == 0), stop=(kc == 7), reads=["wbd"] + hkeys(n * C, C), excl=["psf%d" % bank])
    for bank in range((NCH + 31) // 32):
        CPY(P, "act", bd[:, bank * 32:(bank + 1) * 32, :], psf[bank][0:C, :].rearrange("p (a b) -> p a b", a=32),
            writes=["bd%d" % bank], excl=["psf%d" % bank])
    if NCH <= 32:
        CPY(P, "act", bd[:, 0:1, 0:1], bd[:, 0:1, 0:1], reads=["bd0"], writes=["bd1"])
    bdk = ["bd0", "bd1"]
    bd_b = bd[:, :, 0:8].rearrange("p n h -> p h n")
    bd_a = bd[:, :, 8:16].rearrange("p n h -> p h n")
    ACTV(P, beta, bd_b, AF.Exp, reads=bdk, writes=["beta0"], scale=-1.0)
    ACTV(P, beta, beta, AF.Ln, reads=["beta0"], writes=["beta0"], bias=1.0, scale=1.0)
    ACTV(P, beta, beta, AF.Exp, reads=["beta0"], writes=["beta"], scale=-1.0)
    TS(P, "dve", nbeta, beta, -1.0, None, ALU.mult, reads=["beta"], writes=["nbeta"])
    TS(P, "dve", hbeta, beta, 1.0, None, ALU.mult, reads=["beta"], writes=["hbeta"])
    TT(P, "dve", tmpg, bd_a, bc_inner(dtb, NCH), ALU.add, reads=bdk + ["dtb"], writes=["tmpg"])
    ACTV(P, tmpg, tmpg, AF.Exp, reads=["tmpg"], writes=["tmpg"])
    ACTV(P, tmpg, tmpg, AF.Ln, reads=["tmpg"], writes=["tmpg"], bias=1.0, scale=1.0)
    TT(P, "dve", g, tmpg, bc_inner(nA, NCH), ALU.mult, reads=["tmpg", "nA"], writes=["g"])
    g2 = g.rearrange("p h n -> p (h n)")
    GW = 8 * NCH
    MM(P, psf[2][0:C, 0:GW], tri, g2, reads=["tri", "g"], excl=["psf2"])
    MM(P, psf[3][:, 0:GW], k.ones_f[0:C, :], g2, reads=["ones_f", "g"], excl=["psf3"])
    CPY(P, "act", f2(gc), psf[2][0:C, 0:GW], writes=["gc"], excl=["psf2"])
    ACTV(P, f2(egc), psf[2][0:C, 0:GW], AF.Exp, writes=["egc"], excl=["psf2"])
    CPY(P, "act", f2(glast), psf[3][:, 0:GW], writes=["glast"], excl=["psf3"])
    ACTV(P, f2(egl), psf[3][:, 0:GW], AF.Exp, writes=["egl"], excl=["psf3"])
    TT(P, "dve", edl, glast[0:C], gc, ALU.subtract, reads=["glast", "gc"], writes=["edl0"])
    ACTV(P, edl, edl, AF.Exp, reads=["edl0"], writes=["edl"])
    TT(P, "pool", begc, beta, egc, ALU.mult, reads=["beta", "egc"], writes=["begc"])
    P.barrier()
    A.off = shared_end
    shared = ["tri", "um", "cw", "dnwh", "mhalf", "hbeta", "nbeta", "g", "egc", "edl", "begc", "egl"]
    names = ["dq", "dk", "dv", "dz"]
    id64b = k.ident_b[0:C, 0:C]
    id64f = k.ident_f[0:C, 0:C]
    EPS4 = EPS

    def stream(sid, heads):
        K_ = lambda name: "s%d_%s" % (sid, name)
        wq = [A.alloc(128, (8, 128), BF16) for _ in range(4)]
        dg = A.alloc(128, (12, 128), BF16)
        raw = [[A.alloc(128, (515,), BF16) for _ in range(2)] for _ in range(3)]
        qs = A.alloc(128, (512,), F32)
        ks = A.alloc(128, (512,), F32)
        vs2 = [A.alloc(128, (512,), BF16) for _ in range(2)]
        zs2 = [A.alloc(128, (512,), BF16) for _ in range(2)]
        sq = A.alloc(128, (512,), BF16)
        rn = A.alloc(128, (512,), F32)
        oTb = A.alloc(128, (512,), F32)
        qTb2 = [A.alloc(128, (512,), BF16) for _ in range(2)]
        kTb2 = [A.alloc(128, (512,), BF16) for _ in range(2)]
        qdT2 = [A.alloc(128, (512,), BF16) for _ in range(2)]
        kbg = A.alloc(C, (NCB, 128), BF16)
        kd2 = [A.alloc(C, (NCB, 128), BF16) for _ in range(2)]
        vb = A.alloc(C, (NCB, 128), BF16)
        GU = A.alloc(C, (NCB, C), F32)
        GT = A.alloc(C, (NCB, C), F32)
        Pm = [A.alloc(C, (NCB, C), BF16) for _ in range(2)]
        Qm = [A.alloc(C, (NCB, C), BF16) for _ in range(2)]
        Xm = [A.alloc(C, (NCB, C), BF16) for _ in range(2)]
        dexp = A.alloc(C, (NCB, C), BF16)
        u2 = [A.alloc(C, (NCB, 128), BF16) for _ in range(2)]
        wT2 = [A.alloc(128, (512,), BF16) for _ in range(2)]
        qkT2 = [A.alloc(C, (NCB, C), BF16) for _ in range(2)]
        S_f = A.alloc(128, (128,), F32)
        S_b = A.alloc(128, (128,), BF16)
        vnew = [A.alloc(C, (128,), BF16) for _ in range(2)]
        ob = [A.alloc(128, (512,), BF16) for _ in range(2)]
        tt_ = rn
        Dm, DTm = GU, GT
        base = 4 * sid
        G = [psf[base], psf[base + 1]]
        Gk = [["psf%d" % base], ["psf%d" % (base + 1)]]
        R = [psf[base + 2], psf[base + 3]]
        Rk = [["psf%d" % (base + 2)], ["psf%d" % (base + 3)]]
        prX = [0]
        prY = [0]

        def nextpsX():
            i = prX[0] % 2
            prX[0] += 1
            return R[i], Rk[i]

        def nextpsY():
            i = prY[0] % 2
            prY[0] += 1
            return G[i], Gk[i]

        def silu(dst, dkey, ps, pk_):
            ACTV(P, tt_, ps[:, :], AF.Exp, writes=[K_("rn")], excl=pk_, scale=-1.0)
            ACTV(P, tt_, tt_, AF.Ln, reads=[K_("rn")], writes=[K_("rn")], bias=1.0, scale=1.0)
            ACTV(P, tt_, tt_, AF.Exp, reads=[K_("rn")], writes=[K_("rn")], scale=-1.0)
            TT(P, "dve", dst, tt_, ps[:, :], ALU.mult, reads=[K_("rn")], writes=[dkey], excl=pk_)

        def stageA1(h, b, ui):
            pz = ui % 2
            t0 = b * 512
            rb = b % 2
            hk = hkeys(t0, 512)
            zs, vs, qTb, kTb = zs2[pz], vs2[pz], qTb2[pz], kTb2[pz]
            if b == 0:
                for ti in range(4):
                    wload(k, wq[ti], k.w_in, OFF[names[ti]] + h * 128, 128, 8, K_("wq%d" % ti))
                for ti in range(3):
                    for j in range(4):
                        col = ti * 32 + h * 4 + j
                        TS(P, "dve", dg[:, ti * 4 + j, :], k.ident_f[:], cw[:, col:col + 1], None, ALU.mult,
                           reads=["ident_f", "cw"], writes=[K_("dg%d" % (ti * 4 + j))])
                for ti in range(3):
                    MSET(P, raw[ti][0][:, 0:3], 0.0, writes=[K_("rawh%d_0" % ti)])
                yield
            for ti in range(4):
                ps, pk_ = nextpsX()
                for kc in range(8):
                    MM(P, ps[:, :], wq[ti][:, kc, :], hT[:, kc, t0:t0 + 512], start=(kc == 0), stop=(kc == 7),
                       reads=[K_("wq%d" % ti)] + hk, excl=pk_)
                if ti < 3:
                    CPY(P, "act", raw[ti][rb][:, 3:515], ps[:, :], writes=[K_("raw%d_%d" % (ti, rb))], excl=pk_)
                    CPY(P, "act", raw[ti][1 - rb][:, 0:3], ps[:, 509:512], writes=[K_("rawh%d_%d" % (ti, 1 - rb))], excl=pk_)
                else:
                    silu(zs, K_("zs%d" % pz), ps, pk_)
                yield
            for ti in range(3):
                ps, pk_ = nextpsX()
                for j in range(4):
                    MM(P, ps[:, :], dg[:, ti * 4 + j, :], raw[ti][rb][:, j:j + 512], start=(j == 0), stop=(j == 3),
                       reads=[K_("dg%d" % (ti * 4 + j)), K_("raw%d_%d" % (ti, rb)), K_("rawh%d_%d" % (ti, rb))], excl=pk_)
                silu((qs, ks, vs)[ti], K_(("qs", "ks", "vs%d" % pz)[ti]), ps, pk_)
                yield
            for ti in range(2):
                src = (qs, ks)[ti]
                sk = K_(("qs", "ks")[ti])
                ps, pk_ = nextpsX()
                TT(P, "pool", sq, src, src, ALU.mult, reads=[sk], writes=[K_("sq")])
                MM(P, ps[:, :], k.ones_b[:], sq, reads=["ones_b", K_("sq")], excl=pk_)
                ACTV(P, rn, ps[:, :], AF.Ln, writes=[K_("rn")], excl=pk_, bias=EPS, scale=1.0)
                ACTV(P, rn, rn, AF.Exp, reads=[K_("rn")], writes=[K_("rn")], scale=-0.5)
                if ti == 0:
                    STT(P, "dve", qTb, qs, 128.0 ** -0.5, rn, ALU.mult, ALU.mult, reads=[sk, K_("rn")], writes=[K_("qTb%d" % pz)])
                else:
                    TT(P, "dve", kTb, ks, rn, ALU.mult, reads=[sk, K_("rn")], writes=[K_("kTb%d" % pz)])
                yield

        def stageA2(h, b, ui):
            pz = ui % 2
            nextps = nextpsY
            vs, qTb, kTb = vs2[pz], qTb2[pz], kTb2[pz]
            kTk, qTk, vsk = K_("kTb%d" % pz), K_("qTb%d" % pz), K_("vs%d" % pz)
            kd, u, wT, qkT, qdT = kd2[pz], u2[pz], wT2[pz], qkT2[pz], qdT2[pz]
            cs = slice(b * NCB, (b + 1) * NCB)
            ps, pk_ = nextps()
            psk = ps[:, :].bitcast(BF16)
            for c in range(NCB):
                TRN(P, psk[0:C, c * 128:(c + 1) * 128], kTb[:, c * C:(c + 1) * C], k.ident_b[:], reads=[kTk, "ident_b"], excl=pk_)
            pk = psk[0:C, 0:NCB * 128].rearrange("p (a b) -> p a b", a=NCB)
            TT(P, "dve", kbg, pk, bc_inner(begc[:, h, cs], 128), ALU.mult, reads=["begc"], writes=[K_("kbg")], excl=pk_)
            TT(P, "dve", kd, pk, bc_inner(edl[:, h, cs], 128), ALU.mult, reads=["edl"], writes=[K_("kd%d" % pz)], excl=pk_)
            yield
            ps, pk_ = nextps()
            psv = ps[:, :].bitcast(BF16)
            for c in range(NCB):
                TRN(P, psv[0:C, c * 128:(c + 1) * 128], vs[:, c * C:(c + 1) * C], k.ident_b[:], reads=[vsk, "ident_b"], excl=pk_)
            pv = psv[0:C, 0:NCB * 128].rearrange("p (a b) -> p a b", a=NCB)
            TT(P, "dve", vb, pv, bc_inner(hbeta[:, h, cs], 128), ALU.mult, reads=["hbeta"], writes=[K_("vb")], excl=pk_)
            yield
            gh = g[:, h, cs]
            TT(P, "pool", GU, bc_mid(um, NCB), bc_inner(gh, C), ALU.mult, reads=["um", "g"], writes=[K_("GU")])
            TT(P, "pool", GT, bc_mid(tri, NCB), bc_inner(gh, C), ALU.mult, reads=["tri", "g"], writes=[K_("GT")])
            ps, pk_ = nextps()
            MM(P, ps[0:C, :], tri, f2(GU), reads=["tri", K_("GU")], excl=pk_)
            ACTV(P, f2(Dm), ps[0:C, :], AF.Exp, reads=[K_("GU")], writes=[K_("GU")], excl=pk_)
            yield
            ps, pk_ = nextps()
            MM(P, ps[0:C, :], um, f2(GT), reads=["um", K_("GT")], excl=pk_)
            ACTV(P, f2(DTm), ps[0:C, :], AF.Exp, reads=[K_("GT")], writes=[K_("GT")], excl=pk_)
            TT(P, "pool", Dm, Dm, bc_mid(um, NCB), ALU.mult, reads=[K_("GU"), "um"], writes=[K_("GU")])
            TT(P, "pool", Dm, Dm, bc_inner(nbeta[:, h, cs], C), ALU.mult, reads=[K_("GU"), "nbeta"], writes=[K_("GU")])
            TT(P, "pool", DTm, DTm, bc_mid(tri, NCB), ALU.mult, reads=[K_("GT"), "tri"], writes=[K_("GT")])
            yield
            ps, pk_ = nextps()
            for c in range(NCB):
                MM(P, ps[0:C, c * C:(c + 1) * C], kTb[:, c * C:(c + 1) * C], kTb[:, c * C:(c + 1) * C], reads=[kTk], excl=pk_)
            TT(P, "dve", f2(Qm[0]), ps[0:C, :], f2(Dm), ALU.mult, reads=[K_("GU")], writes=[K_("Q0")], excl=pk_)
            yield
            ps, pk_ = nextps()
            psq = ps[:, :].bitcast(BF16)
            for c in range(NCB):
                TRN(P, psq[0:C, c * C:(c + 1) * C], Qm[0][:, c, :], id64b, reads=[K_("Q0"), "ident_b"], excl=pk_)
            CPY(P, "act", f2(Pm[0]), psq[0:C, 0:512], writes=[K_("P0")], excl=pk_)
            TT(P, "pool", Xm[0], Pm[0], bc_mid(id64b, NCB), ALU.add, reads=[K_("P0"), "ident_b"], writes=[K_("X0")])
            yield

            def PQ(lv):
                a, bcur = (lv - 1) % 2, lv % 2
                Qa, Pa = K_("Q%d" % a), K_("P%d" % a)
                Qb, Pb = K_("Q%d" % bcur), K_("P%d" % bcur)
                if lv < NLV:
                    ps, pk_ = nextps()
                    for c in range(NCB):
                        MM(P, ps[0:C, c * C:(c + 1) * C], Qm[a][:, c, :], Pm[a][:, c, :], reads=[Qa, Pa], excl=pk_)
                    CPY(P, "act", f2(Pm[bcur]), ps[0:C, :], writes=[Pb], excl=pk_)
                ps, pk_ = nextps()
                for c in range(NCB):
                    MM(P, ps[0:C, c * C:(c + 1) * C], Pm[a][:, c, :], Qm[a][:, c, :], reads=[Qa, Pa], excl=pk_)
                CPY(P, "act", f2(Qm[bcur]), ps[0:C, :], writes=[Qb], excl=pk_)

            def XU(lv):
                a, bcur = (lv - 1) % 2, lv % 2
                Qb, Xa, Xb = K_("Q%d" % bcur), K_("X%d" % a), K_("X%d" % bcur)
                ps, pk_ = nextps()
                for c in range(NCB):
                    MM(P, ps[0:C, c * C:(c + 1) * C], Qm[bcur][:, c, :], Xm[a][:, c, :], reads=[Qb, Xa], excl=pk_)
                TT(P, "dve", f2(Xm[bcur]), ps[0:C, :], f2(Xm[a]), ALU.add, reads=[Xa], writes=[Xb], excl=pk_)

            PQ(1)
            yield
            for lv in range(2, NLV + 1):
                PQ(lv)
                yield
                XU(lv - 1)
                yield
            XU(NLV)
            yield
            X = Xm[NLV % 2]
            Xk = K_("X%d" % (NLV % 2))
            for half in range(NCB // 4):
                ps, pk_ = nextps()
                for c in range(4):
                    cc = half * 4 + c
                    MM(P, ps[0:C, c * 128:(c + 1) * 128], X[:, cc, :], vb[:, cc, :], reads=[Xk, K_("vb")], excl=pk_)
                CPY(P, "dve", f2(u[:, half * 4:(half + 1) * 4, :]), ps[0:C, :], writes=[K_("u%d_%d" % (pz, half))], excl=pk_)
                yield
            ps, pk_ = nextps()
            for c in range(NCB):
                MM(P, ps[:, c * C:(c + 1) * C], kbg[:, c, :], X[:, c, :], reads=[Xk, K_("kbg")], excl=pk_)
            CPY(P, "act", wT, ps[:, :], writes=[K_("wT%d" % pz)], excl=pk_)
            yield
            ps, pk_ = nextps()
            for c in range(NCB):
                MM(P, ps[0:C, c * C:(c + 1) * C], kTb[:, c * C:(c + 1) * C], qTb[:, c * C:(c + 1) * C],
                   reads=[kTk, qTk], excl=pk_)
            TT(P, "dve", f2(qkT), ps[0:C, :], f2(DTm), ALU.mult, reads=[K_("GT")], writes=[K_("qkT%d" % pz)], excl=pk_)
            yield
            TT(P, "pool", dexp, bc_mid(id64f, NCB), bc_inner(egc[:, h, cs], C), ALU.mult, reads=["ident_f", "egc"], writes=[K_("dexp")])
            ps, pk_ = nextps()
            MM(P, ps[:, :], k.ones_b[0:C, :], f2(dexp), reads=["ones_b", K_("dexp")], excl=pk_)
            TT(P, "dve", qdT, ps[:, :], qTb, ALU.mult, reads=[qTk], writes=[K_("qdT%d" % pz)], excl=pk_)
            yield

        def stageB(h, b, ui):
            pz = ui % 2
            t0 = b * 512
            zs = zs2[pz]
            kd, u, wT, qkT, qdT = kd2[pz], u2[pz], wT2[pz], qkT2[pz], qdT2[pz]
            if b == 0:
                MSET(P, S_f, 0.0, writes=[K_("S_f")])
                MSET(P, S_b, 0.0, writes=[K_("S_b")])
            for c in range(NCB):
                n = b * NCB + c
                Rb_, Rk_ = R[c % 2], Rk[c % 2]
                vn = vnew[c % 2]
                vk = K_("vnew%d" % (c % 2))
                c64 = slice(c * C, (c + 1) * C)
                oc = slice(256 + (c // 2) * C, 256 + (c // 2) * C + C)
                MM(P, Rb_[0:C, 0:128], wT[:, c64], S_b, reads=[K_("wT%d" % pz), K_("S_b")], excl=Rk_)
                TT(P, "dve", vn, u[:, c, :], Rb_[0:C, 0:128], ALU.subtract, reads=[K_("u%d_%d" % (pz, c // 4))], writes=[vk], excl=Rk_)
                yield
                MM(P, Rb_[:, oc], S_b, qdT[:, c64], start=True, stop=False, reads=[K_("S_b"), K_("qdT%d" % pz)], excl=Rk_)
                MM(P, Rb_[:, oc], vn, qkT[:, c, :], start=False, stop=True, reads=[vk, K_("qkT%d" % pz)], excl=Rk_)
                MM(P, Rb_[:, 128:256], kd[:, c, :], vn, reads=[vk, K_("kd%d" % pz)], excl=Rk_)
                STT(P, "dve", S_b, S_f, egl[:, h, n:n + 1], Rb_[:, 128:256], ALU.mult, ALU.add,
                    reads=[K_("S_f"), "egl"], writes=[K_("S_b")], excl=Rk_)
                STT(P, "dve", S_f, S_f, egl[:, h, n:n + 1], Rb_[:, 128:256], ALU.mult, ALU.add,
                    reads=[K_("S_f"), "egl"], writes=[K_("S_f")], excl=Rk_)
                yield
            o4 = oTb.rearrange("p (a t c) -> p a t c", a=NCB // 2, t=2)
            for par in range(2):
                CPY(P, "act", o4[:, :, par, :], R[par][:, 256:512].rearrange("p (a c) -> p a c", a=NCB // 2),
                    reads=[K_("oTb")], writes=[K_("oTb")], excl=Rk[par])
            yield
            ps, pk_ = nextpsX()
            TT(P, "pool", sq, oTb, oTb, ALU.mult, reads=[K_("oTb")], writes=[K_("sq")])
            MM(P, ps[:, :], k.ones_b[:], sq, reads=["ones_b", K_("sq")], excl=pk_)
            ACTV(P, rn, ps[:, :], AF.Ln, writes=[K_("rn")], excl=pk_, bias=EPS, scale=1.0 / 128)
            ACTV(P, rn, rn, AF.Exp, reads=[K_("rn")], writes=[K_("rn")], scale=-0.5)
            yield
            TT(P, "dve", oTb, oTb, rn, ALU.mult, reads=[K_("oTb"), K_("rn")], writes=[K_("oTb")])
            o_ = ob[b % 2]
            okey = K_("ob%d" % (b % 2))
            STT(P, "dve", o_, oTb, dnwh[:, 0:1], zs, ALU.mult, ALU.mult, reads=[K_("oTb"), "dnwh", K_("zs%d" % pz)], writes=[okey])
            DMA(P, "sp", k.oaT_d[h, :, t0:t0 + 512], o_, reads=[okey], writes=["oaT_d"])
            if k.debug:
                DMA(P, "sp", k.dbg_oa[h, :, t0:t0 + 512], o_, reads=[okey])
            yield

        def seq(*gens):
            for gg in gens:
                yield from gg

        def merge(g1_, g2_):
            act_ = [g1_, g2_]
            while act_:
                for gg in list(act_):
                    try:
                        next(gg)
                        yield
                    except StopIteration:
                        act_.remove(gg)

        units = [(h, b) for h in heads for b in range(8)]
        NU = len(units)
        yield from stageA1(units[0][0], units[0][1], 0)
        yield from stageA1(units[1][0], units[1][1], 1)
        yield from stageA2(units[0][0], units[0][1], 0)
        for ui in range(NU):
            xs_ = [stageB(units[ui][0], units[ui][1], ui)]
            if ui + 2 < NU:
                xs_.append(stageA1(units[ui + 2][0], units[ui + 2][1], ui + 2))
            if ui + 1 < NU:
                yield from merge(seq(*xs_), stageA2(units[ui + 1][0], units[ui + 1][1], ui + 1))
            else:
                yield from seq(*xs_)

    g0 = stream(0, [0, 2, 4, 6])
    g1 = stream(1, [1, 3, 5, 7])
    run_streams([g0, g1], lead=20)


def phase2b(k):
    P, A = k.P, k.A
    psf = k.psf
    hT = k.hT
    hTk = ["hT%d" % i for i in range(32)]
    cos = A.alloc(32, (T,), F32)
    sin = A.alloc(32, (T,), F32)
    maskb = A.alloc(128, (256,), BF16)
    pmat = A.alloc(32, (32,), BF16)
    DMA(P, "sp", cos, k.c_cos, writes=["cos"])
    DMA(P, "sp", sin, k.c_sin, writes=["sin"])
    DMA(P, "pool", maskb, k.c_maskb, writes=["maskb"])
    DMA(P, "pool", pmat, k.c_pm, writes=["pmat"])
    w3 = [[A.alloc(128, (8, 128), BF16) for _ in range(3)] for _ in range(2)]
    qTp = [A.alloc(128, (T,), BF16) for _ in range(2)]
    kTp = [A.alloc(128, (T,), BF16) for _ in range(2)]
    vp = [A.alloc(128, (32, 128), BF16) for _ in range(2)]
    qraw = A.alloc(128, (512,), BF16)
    t1 = A.alloc(32, (512,), F32)
    t2 = A.alloc(32, (512,), F32)
    PT = [A.alloc(128, (256,), BF16) for _ in range(3)]
    num = A.alloc(128, (T,), F32)
    den = A.alloc(128, (T,), F32)
    obs = [A.alloc(128, (512,), BF16) for _ in range(2)]
    scale = 128.0 ** -0.5
    names = ["aq", "ak", "av"]
    heads = [(hs, gi) for hs in range(4) for gi in range(3)]

    def perm_view(buf, d, t0, p0, p1):
        m0 = t0 // d
        v = buf[p0:p1, :].rearrange("p (r m) -> p m r", r=d)
        return v[:, m0:m0 + 512 // d, :]

    def nat_view(buf, d, pos0):
        M = T // d
        if d == 1:
            return buf[:, pos0:pos0 + 512]
        if d == 4:
            r = pos0 // M
            m0 = pos0 % M
            return buf[:, :].rearrange("p (m r) -> p r m", r=4)[:, r, m0:m0 + 512]
        r0 = pos0 // M
        return buf[:, :].rearrange("p (m r) -> p r m", r=16)[:, r0:r0 + 2, :]

    def qkkeys(s):
        return ["%s%d%s%d" % (n_, s, s_, b) for n_ in ("qTp", "kTp") for s_ in ("b",) for b in range(8)]

    def proj_stage(i):
        hs, gi = heads[i]
        d = GROUPS[gi][1]
        hh = gi * 4 + hs
        s = i % 2
        nbr = (T // d) // 128
        w = w3[s]
        for ti in range(3):
            wload(k, w[ti], k.w_in, OFF[names[ti]] + hh * 128, 128, 8, "w3_%d_%d" % (s, ti))
        pend = None

        def rope(b, ti, ps, pk_):
            t0 = b * 512
            dstb = (qTp[s], kTp[s])[ti]
            dk_ = ("qTp%d" % s, "kTp%d" % s)[ti]
            dst = dstb[:, t0:t0 + 512]
            CPY(P, "act", dst, ps[:, :], writes=[dk_ + "b%d" % b], excl=pk_)
            MM(P, psf[2][0:32, :], pmat, dstb[0:32, t0:t0 + 512], reads=["pmat", dk_ + "b%d" % b], excl=["psf2"])
            TT(P, "dve", t1, ps[0:32, :], cos[:, t0:t0 + 512], ALU.mult, reads=["cos"], writes=["t1"], excl=pk_)
            TT(P, "dve", t2, psf[2][0:32, :], sin[:, t0:t0 + 512], ALU.mult, reads=["sin"], writes=["t2"], excl=["psf2"])
            TT(P, "pool", dstb[0:32, t0:t0 + 512], t1, t2, ALU.add, reads=["t1", "t2", dk_ + "b%d" % b], writes=[dk_ + "b%d" % b])

        u_ = 0
        for b in range(8):
            t0 = b * 512
            hk = hTk[b * 4:(b + 1) * 4]
            for ti in range(2):
                pi = u_ % 2
                u_ += 1
                ps = psf[pi]
                pk_ = ["psf%d" % pi]
                for kc in range(8):
                    MM(P, ps[:, :], w[ti][:, kc, :], hT[:, kc, t0:t0 + 512], start=(kc == 0), stop=(kc == 7),
                       reads=["w3_%d_%d" % (s, ti)] + hk, excl=pk_)
                if pend is not None:
                    rope(*pend)
                pend = (b, ti, ps, pk_)
                yield
        rope(*pend)
        yield
        for j in range(32):
            r = j // nbr
            n = j % nbr
            base = r + d * 128 * n
            for kc in range(8):
                hv = hT[:, kc, :]
                lh = bass.AP(hv.tensor, hv.offset + base, [list(hv.ap[0]), [d, 128]])
                MM(P, psf[3][:, (j % 4) * 128:(j % 4 + 1) * 128], lh, w[2][:, kc, :], start=(kc == 0), stop=(kc == 7),
                   reads=["w3_%d_2" % s] + hTk, excl=["psf3"])
            if j % 4 == 3:
                CPY(P, "act", f2(vp[s][:, j - 3:j + 1, :]), psf[3][:, :], writes=["vp%d_%d" % (s, j // 4)], excl=["psf3"])
            if j % 2 == 1:
                yield

    def core_stage(i):
        hs, gi = heads[i]
        d = GROUPS[gi][1]
        s = i % 2
        nbr = (T // d) // 128
        qk_keys = qkkeys(s)
        vkeys = ["vp%d_%d" % (s, x) for x in range(8)]
        q_, k_, v_ = qTp[s], kTp[s], vp[s]

        def front(j):
            r = j // nbr
            n = j % nbr
            W = 256 if n < nbr - 1 else 128
            pi = 4 + j % 2
            ps = psf[pi]
            pk_ = ["psf%d" % pi]
            pt = PT[j % 3]
            base = r + d * 128 * n
            kk_ = bass.AP(k_.tensor, k_.offset + base, [list(k_.ap[0]), [d, 128]])
            qq_ = bass.AP(q_.tensor, q_.offset + base, [list(q_.ap[0]), [d, W]])
            MM(P, ps[:, 0:W], kk_, qq_, start=True, stop=True, reads=qk_keys, excl=pk_)
            ACTV(P, pt[:, 0:W], ps[:, 0:W], AF.Exp, writes=["PT%d" % (j % 3)], excl=pk_, scale=scale)
            TT(P, "dve", pt[:, 0:W], pt[:, 0:W], maskb[:, 0:W], ALU.mult, reads=["PT%d" % (j % 3), "maskb"], writes=["PT%d" % (j % 3)])

        front(0)
        for j in range(32):
            if j + 1 < 32:
                front(j + 1)
            n = j % nbr
            pt = PT[j % 3]
            ptk = "PT%d" % (j % 3)
            col = (j % 4) * 128
            prevk = "PT%d" % ((j - 1) % 3)
            prev = PT[(j - 1) % 3]
            for which in range(2):
                pacc = 6 + which
                lh_prev = v_[:, max(j - 1, 0), :] if which == 0 else k.ones_b[:]
                lh_cur = v_[:, j, :] if which == 0 else k.ones_b[:]
                if n > 0:
                    MM(P, psf[pacc][:, col:col + 128], lh_prev, prev[:, 128:256], start=True, stop=False,
                       reads=vkeys + [prevk, "ones_b"], excl=["psf%d" % pacc])
                MM(P, psf[pacc][:, col:col + 128], lh_cur, pt[:, 0:128], start=(n == 0), stop=True,
                   reads=vkeys + [ptk, "ones_b"], excl=["psf%d" % pacc])
            if j % 4 == 3:
                pos0 = (j // 4) * 512
                nv = nat_view(num, d, pos0)
                dv_ = nat_view(den, d, pos0)
                if d == 16:
                    s6 = psf[6][:, :].rearrange("p (r m) -> p r m", r=2)
                    s7 = psf[7][:, :].rearrange("p (r m) -> p r m", r=2)
                else:
                    s6 = psf[6][:, :]
                    s7 = psf[7][:, :]
                if gi == 0:
                    CPY(P, "act", nv, s6, writes=["num"], excl=["psf6"])
                    CPY(P, "dve", dv_, s7, writes=["den"], excl=["psf7"])
                else:
                    TT(P, "dve", nv, nv, s6, ALU.add, reads=["num"], writes=["num"], excl=["psf6"])
                    TT(P, "dve", dv_, dv_, s7, ALU.add, reads=["den"], writes=["den"], excl=["psf7"])
            yield
        if gi == 2:
            RCP(P, den, den, reads=["den"], writes=["den"])
            for b in range(8):
                o_ = obs[b % 2]
                ok_ = "obs%d" % (b % 2)
                TT(P, "dve", o_, num[:, b * 512:(b + 1) * 512], den[:, b * 512:(b + 1) * 512], ALU.mult,
                   reads=["num", "den"], writes=[ok_])
                DMA(P, "sp", k.obT_d[hs, :, b * 512:(b + 1) * 512], o_, reads=[ok_], writes=["obT_d"])
                if k.debug:
                    DMA(P, "sp", k.dbg_ob[hs, :, b * 512:(b + 1) * 512], o_, reads=[ok_])
            yield

    run_streams([proj_stage(0)])
    for i in range(12):
        gens = [core_stage(i)]
        if i + 1 < 12:
            gens.append(proj_stage(i + 1))
        run_streams(gens)


def phase3a(k):
    P, A = k.P, k.A
    psf = k.psf
    hT = k.hT
    hTk = ["hT%d" % i for i in range(32)]
    wga = A.alloc(128, (8, D), BF16)
    wgb = A.alloc(128, (8, D), BF16)
    wpa = A.alloc(128, (8, D), BF16)
    wpb = A.alloc(128, (4, D), BF16)
    wo = A.alloc(128, (8, D), BF16)
    for cc in range(8):
        c0 = cc * 128
        wload(k, wpa[:, :, c0:c0 + 128], k.w_proj_a, c0, 128, 8, "wpa%d" % cc)
        wload(k, wpb[:, :, c0:c0 + 128], k.w_proj_b, c0, 128, 4, "wpb%d" % cc)
        wload(k, wga[:, :, c0:c0 + 128], k.w_in, OFF["ga"] + c0, 128, 8, "wga%d" % cc)
        wload(k, wgb[:, :, c0:c0 + 128], k.w_in, OFF["gb"] + c0, 128, 8, "wgb%d" % cc)
    wload(k, wo, k.w_out, 0, D, 8, "wo")
    junk3 = A.alloc(128, (D,), BF16)
    oa = [A.alloc(128, (8, 512), BF16) for _ in range(2)]
    obt = [A.alloc(128, (4, 512), BF16) for _ in range(2)]
    sga2 = [A.alloc(128, (512,), F32) for _ in range(2)]
    sgb2 = [A.alloc(128, (512,), F32) for _ in range(2)]
    m1 = A.alloc(128, (512,), F32)
    m2 = A.alloc(128, (512,), F32)
    mg = A.alloc(128, (8, 512), BF16)
    xt = [A.alloc(128, (D,), F32) for _ in range(2)]
    for b in range(8):
        t0 = b * 512
        s = b % 2
        hk = hTk[b * 4:(b + 1) * 4]
        DMA(P, "sp", oa[s], k.oaT_d[:, :, t0:t0 + 512].rearrange("h p t -> p h t"), reads=["oaT_d"], writes=["oa%d" % s])
        DMA(P, "sp", obt[s], k.obT_d[:, :, t0:t0 + 512].rearrange("h p t -> p h t"), reads=["obT_d"], writes=["obt%d" % s])
        for cc in range(8):
            cs = slice(cc * 128, (cc + 1) * 128)
            o_ = 4 * (cc % 2)
            pA, pB, pGa, pGb = psf[o_], psf[o_ + 1], psf[o_ + 2], psf[o_ + 3]
            kA, kB, kGa, kGb = ["psf%d" % o_], ["psf%d" % (o_ + 1)], ["psf%d" % (o_ + 2)], ["psf%d" % (o_ + 3)]
            for kc in range(8):
                MM(P, pGa[:, :], wga[:, kc, cs], hT[:, kc, t0:t0 + 512], start=(kc == 0), stop=(kc == 7), reads=["wga%d" % cc] + hk, excl=kGa)
            for kc in range(8):
                MM(P, pGb[:, :], wgb[:, kc, cs], hT[:, kc, t0:t0 + 512], start=(kc == 0), stop=(kc == 7), reads=["wgb%d" % cc] + hk, excl=kGb)
            for kc in range(8):
                MM(P, pA[:, :], wpa[:, kc, cs], oa[s][:, kc, :], start=(kc == 0), stop=(kc == 7), reads=["wpa%d" % cc, "oa%d" % s], excl=kA)
            for kc in range(4):
                MM(P, pB[:, :], wpb[:, kc, cs], obt[s][:, kc, :], start=(kc == 0), stop=(kc == 3), reads=["wpb%d" % cc, "obt%d" % s], excl=kB)
            sa, sb_ = sga2[cc % 2], sgb2[cc % 2]
            ACTV(P, sa, pGa[:, :], AF.Sigmoid, writes=["sga%d" % (cc % 2)], excl=kGa)
            ACTV(P, sb_, pGb[:, :], AF.Sigmoid, writes=["sgb%d" % (cc % 2)], excl=kGb)
            TT(P, "dve", m1, pA[:, :], sa, ALU.mult, reads=["sga%d" % (cc % 2)], writes=["m1"], excl=kA)
            TT(P, "dve", m2, pB[:, :], sb_, ALU.mult, reads=["sgb%d" % (cc % 2)], writes=["m2"], excl=kB)
            TT(P, "pool", mg[:, cc, :], m1, m2, ALU.add, reads=["m1", "m2"], writes=["mg%d" % cc])
        mgk = ["mg%d" % i for i in range(8)]
        for tt in range(4):
            tok = t0 + tt * 128
            xs = (b * 4 + tt) % 2
            xk = "x3t%d" % xs
            DMA(P, "sp", xt[xs], k.x[tok:tok + 128, :], writes=[xk])
            for half in range(2):
                pi = 2 * (tt % 2) + half
                hs_ = slice(half * 512, (half + 1) * 512)
                for cc in range(8):
                    MM(P, psf[pi][:, :], mg[:, cc, tt * 128:(tt + 1) * 128], wo[:, cc, hs_], start=(cc == 0), stop=(cc == 7),
                       reads=mgk + ["wo"], excl=["psf%d" % pi])
                TT(P, "dve", xt[xs][:, hs_], psf[pi][:, :], xt[xs][:, hs_], ALU.add, reads=[xk], writes=[xk], excl=["psf%d" % pi])
            DMA(P, "sp", k.x1_d[tok:tok + 128, :], xt[xs], reads=[xk], writes=["x1_d"])
            ti_ = b * 4 + tt
            ACTV(P, junk3, xt[xs], AF.Square, reads=[xk], writes=["junk3", "stat2_%d" % ti_], accum_out=k.stat2[:, ti_:ti_ + 1])
            if k.debug:
                DMA(P, "sp", k.dbg_x1[tok:tok + 128, :], xt[xs], reads=[xk])


def phase3b(k):
    P, A = k.P, k.A
    psf = k.psf
    st = k.stat
    ACTV(P, k.stat2[:, 32:64], k.stat2[:, 0:32], AF.Sqrt, writes=["rstd2a"], scale=1.0 / D, bias=EPS)
    RCP(P, k.stat2[:, 32:64], k.stat2[:, 32:64], reads=["rstd2a"], writes=["rstd2"])
    wd = A.alloc(128, (22, D), BF16)
    wload(k, wd, k.w_down, 0, D, 22, "wd")
    fw_bc = A.alloc(128, (D,), F32)
    DMA(P, "sp", fw_bc, bass.AP(k.final_norm_w.tensor, 0, [[0, 128], [1, D]]), writes=["fw_bc"])
    DMA(P, "sp", k.nw_bc[:], bass.AP(k.norm2_w.tensor, 0, [[0, 128], [1, D]]), writes=["nw2"])
    h2T = [A.alloc(128, (8, 1024), BF16) for _ in range(2)]
    actT = A.alloc(128, (22, 1024), BF16)
    wgu = [[A.alloc(128, (8, 512), BF16) for _ in range(2)] for _ in range(2)]
    sg = [A.alloc(128, (512,), F32) for _ in range(2)]
    x1t = [A.alloc(128, (D,), F32) for _ in range(2)]
    xt = [A.alloc(128, (D,), F32) for _ in range(2)]
    xn = [A.alloc(128, (D,), BF16) for _ in range(2)]
    junk = A.alloc(128, (D,), BF16)
    groups = [(0, 4), (4, 4), (8, 4), (12, 4), (16, 4), (20, 2)]
    ntc = [0]

    def norm_tile(sti, i):
        s_ = ntc[0] % 2
        ntc[0] += 1
        tile = sti * 8 + i
        tok = tile * 128
        dst = h2T[sti % 2][:, :, i * 128:(i + 1) * 128]
        DMA(P, "sp", xt[s_], k.x1_d[tok:tok + 128, :], reads=["x1_d"], writes=["nxt%d" % s_])
        STT(P, "dve", xn[s_], xt[s_], k.stat2[:, 32 + tile:33 + tile], k.nw_bc[:], ALU.mult, ALU.mult,
            reads=["nxt%d" % s_, "rstd2", "nw2"], writes=["nxn%d" % s_])
        pb = k.psb[s_]
        for kc in range(8):
            TRN(P, pb[:, kc * 128:(kc + 1) * 128], xn[s_][:, kc * 128:(kc + 1) * 128], k.ident_b[:],
                reads=["nxn%d" % s_, "ident_b"], excl=["psb%d" % s_])
        CPY(P, "act", dst, pb[:].rearrange("p (a b) -> p a b", a=8), writes=["h2T%d_%d" % (sti % 2, i)], excl=["psb%d" % s_])

    for i in range(8):
        norm_tile(0, i)
    gcount = 0
    for sti in range(4):
        tok0 = sti * 1024
        hb = sti % 2
        hkeys_ = ["h2T%d_%d" % (hb, i) for i in range(8)]
        nxt_tiles = list(range(8)) if sti + 1 < 4 else []
        for (f0, nf) in groups:
            ws = gcount % 2
            gcount += 1
            gsrc = k.w_gate_up[:, f0 * 128:(f0 + nf) * 128].rearrange("(kc p) c -> p kc c", p=128)
            usrc = k.w_gate_up[:, DFF + f0 * 128:DFF + (f0 + nf) * 128].rearrange("(kc p) c -> p kc c", p=128)
            DMA(P, "pool", wgu[ws][0][:, :, 0:nf * 128], gsrc, writes=["wgu%da" % ws])
            DMA(P, "pool", wgu[ws][1][:, :, 0:nf * 128], usrc, writes=["wgu%db" % ws])
            for fi in range(nf):
                fc = f0 + fi
                fsl = slice(fi * 128, (fi + 1) * 128)
                for tb in range(2):
                    pg, pu = (0, 1) if tb == 0 else (2, 3)
                    hk = hkeys_[tb * 4:(tb + 1) * 4]
                    tsl = slice(tb * 512, (tb + 1) * 512)
                    for kc in range(8):
                        MM(P, psf[pg][:, :], wgu[ws][0][:, kc, fsl], h2T[hb][:, kc, tsl], start=(kc == 0), stop=(kc == 7),
                           reads=["wgu%da" % ws] + hk, excl=["psf%d" % pg])
                    for kc in range(8):
                        MM(P, psf[pu][:, :], wgu[ws][1][:, kc, fsl], h2T[hb][:, kc, tsl], start=(kc == 0), stop=(kc == 7),
                           reads=["wgu%db" % ws] + hk, excl=["psf%d" % pu])
                    ACTV(P, sg[tb], psf[pg][:, :], AF.Silu, writes=["sg%d" % tb], excl=["psf%d" % pg])
                    TT(P, "dve", actT[:, fc, tsl], psf[pu][:, :], sg[tb], ALU.mult, reads=["sg%d" % tb],
                       writes=["actT%d_%d" % (fc, tb)], excl=["psf%d" % pu])
                if nxt_tiles and fc % 3 == 1 or (nxt_tiles and fc == 21):
                    norm_tile(sti + 1, nxt_tiles.pop(0))
        while nxt_tiles:
            norm_tile(sti + 1, nxt_tiles.pop(0))
        for tt in range(8):
            tok = tok0 + tt * 128
            xs = tt % 2
            xk = "x1t%d" % xs
            ak = ["actT%d_%d" % (fc, tt // 4) for fc in range(22)]
            DMA(P, "sp", x1t[xs], k.x1_d[tok:tok + 128, :], reads=["x1_d"], writes=[xk])
            for half in range(2):
                pi = 4 + half
                hs_ = slice(half * 512, (half + 1) * 512)
                for fc in range(22):
                    MM(P, psf[pi][:, :], actT[:, fc, tt * 128:(tt + 1) * 128], wd[:, fc, hs_], start=(fc == 0), stop=(fc == 21),
                       reads=ak + ["wd"], excl=["psf%d" % pi])
                TT(P, "dve", x1t[xs][:, hs_], psf[pi][:, :], x1t[xs][:, hs_], ALU.add, reads=[xk], writes=[xk], excl=["psf%d" % pi])
            sk = "fst%d" % xs
            ACTV(P, junk, x1t[xs], AF.Square, reads=[xk], writes=["fjunk", sk], accum_out=st[:, 48 + xs:49 + xs])
            ACTV(P, st[:, 50 + xs:51 + xs], st[:, 48 + xs:49 + xs], AF.Sqrt, reads=[sk], writes=[sk + "b"], scale=1.0 / D, bias=EPS)
            RCP(P, st[:, 52 + xs:53 + xs], st[:, 50 + xs:51 + xs], reads=[sk + "b"], writes=[sk + "c"])
            STT(P, "dve", x1t[xs], x1t[xs], st[:, 52 + xs:53 + xs], fw_bc, ALU.mult, ALU.mult,
                reads=[xk, sk + "c", "fw_bc"], writes=[xk])
            DMA(P, "sp", k.out[tok:tok + 128, :], x1t[xs], reads=[xk])


def make_consts():
    c = {}
    c["c_ident"] = np.eye(128, dtype=np.float32)
    kj = np.arange(128)[:, None]
    qi = np.arange(128)[None, :]
    mb = np.zeros((128, 256), np.float32)
    mb[:, 0:128] = np.where(kj <= qi, 1.0, 0.0)
    mb[:, 128:256] = np.where(kj >= qi, 1.0, 0.0)
    c["c_maskb"] = mb
    half = 16
    inv_freq = np.power(np.float32(500000.0), -np.arange(half, dtype=np.float32) * np.float32(2.0 / 32)).astype(np.float32)
    ang = (np.arange(T, dtype=np.float32)[:, None] * inv_freq[None, :]).astype(np.float32)
    cos = np.cos(ang).astype(np.float32).T
    sin = np.sin(ang).astype(np.float32).T
    c["c_cos"] = np.ascontiguousarray(np.concatenate([cos, cos], 0))
    c["c_sin"] = np.ascontiguousarray(np.concatenate([sin, sin], 0))
    pm = np.zeros((32, 32), np.float32)
    for i in range(16):
        pm[i + 16, i] = -1.0
        pm[i, i + 16] = 1.0
    c["c_pm"] = pm
    j = np.arange(CH)[:, None]
    cc = np.arange(CH)[None, :]
    c["c_tri"] = (j <= cc).astype(np.float32)
    c["c_u"] = (j > cc).astype(np.float32)
    return c


def make_in_maps(inputs, ncores=8):
    f = lambda a: np.ascontiguousarray(np.asarray(a, dtype=np.float32))
    shared = {
        "norm1_w": f(inputs["norm1_w"]).reshape(1, D),
        "w_in": f(inputs["w_in"])[0],
        "cwT": np.ascontiguousarray(f(inputs["conv_w"])[0].reshape(4, 3, 8, 128).transpose(3, 1, 2, 0).reshape(128, 96)),
        "a_log": f(inputs["a_log"]).reshape(1, 8),
        "dt_bias": f(inputs["dt_bias"]).reshape(1, 8),
        "dn_norm_w": f(inputs["dn_norm_w"]).reshape(128, 1),
        "w_proj_a": f(inputs["w_proj_a"])[0],
        "w_proj_b": f(inputs["w_proj_b"])[0],
        "w_out": f(inputs["w_out"])[0],
        "norm2_w": f(inputs["norm2_w"]).reshape(1, D),
        "w_gate_up": f(inputs["w_gate_up"])[0],
        "w_down": f(inputs["w_down"])[0],
        "final_norm_w": f(inputs["final_norm_w"]).reshape(1, D),
    }
    shared.update(make_consts())
    x = f(inputs["x"])
    maps = []
    for c in range(ncores):
        m = dict(shared)
        m["x"] = x[c]
        maps.append(m)
    return maps


_CACHE = {}


def kernel(**inputs):
    if "nc" not in _CACHE:
        _CACHE["nc"] = build_program()[0]
    nc = _CACHE["nc"]
    maps = make_in_maps(inputs, 8)
    res = run_bass_kernel_spmd(nc, maps, core_ids=list(range(8)))
    return np.stack([np.asarray(r["out"]) for r in res.results], axis=0).astype(np.float32)
```

```python
import contextlib
import numpy as np
import concourse.bass as bass
import concourse.mybir as mybir
from concourse.bass_utils import run_bass_kernel_spmd

F32 = mybir.dt.float32
BF16 = mybir.dt.bfloat16
ALU = mybir.AluOpType
AF = mybir.ActivationFunctionType

T = 4096
D = 1024
IN_DIM = 10768
DFF = 2816
EPS = 1e-6
OFF = dict(dq=0, dk=1024, dv=2048, dz=3072, db=4096, da=4104, aq=4112, ak=5648, av=7184,
           ga=8720, gb=9744)
GROUPS = ((128, 1), (512, 4), (2048, 16))
CH = 128


class Ins:
    __slots__ = ("eng", "fn", "deps", "odeps", "dma", "need", "sem", "val", "cost", "seg", "idx",
                 "prio", "nrem", "succ", "fin", "waits")

    def __init__(self, eng, fn, dma):
        self.eng = eng
        self.fn = fn
        self.deps = []
        self.odeps = []
        self.dma = dma
        self.need = dma
        self.sem = None
        self.val = 0
        self.cost = 0.5
        self.seg = 0


EMBED_DMA = False


class Prog:
    ENGS = ("pe", "act", "dve", "pool", "sp")

    def __init__(self, nc, n_dma_sems=42):
        self.nc = nc
        self.lists = {e: [] for e in self.ENGS}
        self.bufs = {}
        self.excl = {}
        self.n_dma_sems = n_dma_sems
        self.order = []
        self.seg = 0

    def emit(self, eng, fn, reads=(), writes=(), excl=(), dma=False, cost=0.5):
        ins = Ins(eng, fn, dma)
        ins.cost = cost
        ins.seg = self.seg
        deps = {}
        for k in reads:
            st = self.bufs.get(k)
            if st is None:
                st = self.bufs[k] = [None, []]
            if st[0] is not None:
                deps[id(st[0])] = st[0]
        for k in writes:
            st = self.bufs.get(k)
            if st is None:
                st = self.bufs[k] = [None, []]
            if st[0] is not None:
                deps[id(st[0])] = st[0]
            for r in st[1]:
                deps[id(r)] = r
        odeps_extra = []
        for k in excl:
            st = self.excl.get(k)
            if st is None:
                st = self.excl[k] = {}
            for e2, i2 in st.items():
                if e2 != eng:
                    deps[id(i2)] = i2
                else:
                    odeps_extra.append(i2)
            st[eng] = ins
        for k in reads:
            self.bufs[k][1].append(ins)
        for k in writes:
            self.bufs[k] = [ins, []]
        dl = []
        for d in deps.values():
            if d is ins:
                continue
            if d.eng == "pe" and eng == "pe" and not d.dma and not dma:
                continue
            dl.append(d)
        ins.deps = dl
        ins.odeps = [d for d in deps.values() if d is not ins] + [d for d in odeps_extra if d is not ins]
        self.lists[eng].append(ins)
        self.order.append(ins)
        return ins

    def barrier(self):
        self.seg += 1
        self.bufs = {}
        self.excl = {}

    def schedule(self):
        LAT = 0.5
        segs = {}
        for ins in self.order:
            segs.setdefault(ins.seg, []).append(ins)
        new_order = []
        prev_lasts = []
        for sg in sorted(segs):
            L = segs[sg]
            inseg = set(id(i) for i in L)
            for i in L:
                i.succ = []
                i.nrem = 0
            for i in L:
                seen_ = set()
                for d in i.odeps:
                    if id(d) in inseg and id(d) not in seen_:
                        seen_.add(id(d))
                        d.succ.append(i)
                        i.nrem += 1
            for i in reversed(L):
                m = 0.0
                for s_ in i.succ:
                    if s_.prio > m:
                        m = s_.prio
                i.prio = m + i.cost + LAT
            ready = {e: [] for e in self.ENGS}
            for i in L:
                i.fin = 0.0
                if i.nrem == 0:
                    ready[i.eng].append(i)
            rdy_t = {}
            tfree = {e: 0.0 for e in self.ENGS}
            out = []
            nleft = len(L)
            while nleft:
                best = None
                for e in self.ENGS:
                    rl = ready[e]
                    if not rl:
                        continue
                    bi = None
                    for c in rl:
                        st_ = rdy_t.get(id(c), 0.0)
                        if st_ < tfree[e]:
                            st_ = tfree[e]
                        key = (st_, -c.prio)
                        if bi is None or key < bi[0]:
                            bi = (key, c)
                    if best is None or bi[0] < best[0]:
                        best = bi
                (st_, _), c = best
                ready[c.eng].remove(c)
                if c.dma:
                    tfree[c.eng] = st_ + (1.5 if c.eng == "pool" else 0.1)
                    c.fin = st_ + c.cost
                else:
                    tfree[c.eng] = st_ + c.cost
                    c.fin = st_ + c.cost
                out.append(c)
                nleft -= 1
                for s_ in c.succ:
                    t_ = c.fin + (0.0 if (s_.eng == c.eng and not c.dma) else LAT)
                    if rdy_t.get(id(s_), 0.0) < t_:
                        rdy_t[id(s_)] = t_
                    s_.nrem -= 1
                    if s_.nrem == 0:
                        ready[s_.eng].append(s_)
            if prev_lasts:
                firsts = {}
                for i in out:
                    if i.eng not in firsts:
                        firsts[i.eng] = i
                for i in firsts.values():
                    i.deps = list(i.deps) + [d for d in prev_lasts if d is not i]
            lasts = []
            for e in self.ENGS:
                for i in reversed(out):
                    if i.eng == e and not i.dma:
                        lasts.append(i)
                        break
            dm = [i for i in out if i.dma]
            lasts.extend([i for i in dm if i.eng == "pool"][-self.n_dma_sems:])
            lasts.extend([i for i in dm if i.eng != "pool"][-self.n_dma_sems:])
            if not lasts:
                lasts = prev_lasts
            prev_lasts = lasts
            new_order.extend(out)
        self.order = new_order
        self.lists = {e: [i for i in new_order if i.eng == e] for e in self.ENGS}

    def finalize(self, stack):
        nc = self.nc
        self.schedule()
        for e in self.ENGS:
            for n_, ins in enumerate(self.lists[e]):
                ins.idx = n_
        for ins in self.order:
            best = {}
            keep = []
            for d in ins.deps:
                if d.dma:
                    keep.append(d)
                elif d.eng not in best or best[d.eng].idx < d.idx:
                    best[d.eng] = d
            ins.deps = keep + list(best.values())
            for d in ins.deps:
                d.need = True
        esem = {e: stack.enter_context(nc.semaphore("s_" + e)) for e in self.ENGS}
        dsems = [stack.enter_context(nc.semaphore("d%d" % i)) for i in range(self.n_dma_sems)]
        cnt = {e: 0 for e in self.ENGS}
        dcnt = [0] * self.n_dma_sems
        dlast = [None] * self.n_dma_sems
        n_sw = self.n_dma_sems // 3
        n_hw = self.n_dma_sems - n_sw
        nd_hw = 0
        nd_sw = 0
        for ins in self.order:
            if ins.dma:
                if ins.eng == "pool":
                    s = n_hw + (nd_sw % n_sw)
                    nd_sw += 1
                else:
                    s = nd_hw % n_hw
                    nd_hw += 1
                if dlast[s] is not None:
                    ins.deps.append(dlast[s])
                dcnt[s] += 16
                ins.sem = dsems[s]
                ins.val = dcnt[s]
                dlast[s] = ins
            elif ins.need:
                cnt[ins.eng] += 1
                ins.sem = esem[ins.eng]
                ins.val = cnt[ins.eng]
        self.counts = dict(cnt)
        fin = [d for d in dlast if d is not None]
        lists = self.lists
        nwaits = {e: 0 for e in self.ENGS}
        known = {e: {} for e in self.ENGS}
        vc = {}
        for ins in self.order:
            kn = known[ins.eng]
            cand = {}
            for d in ins.deps:
                key = id(d.sem)
                if kn.get(key, 0) < d.val and (key not in cand or cand[key].val < d.val):
                    cand[key] = d
            wl = sorted(cand.values(), key=lambda d: -getattr(d, "fin", 0.0))
            kept = []
            for d in wl:
                key = id(d.sem)
                if kn.get(key, 0) >= d.val:
                    continue
                kept.append((d.sem, d.val))
                kn[key] = d.val
                for k2, v2 in vc[id(d)].items():
                    if kn.get(k2, 0) < v2:
                        kn[k2] = v2
            ins.waits = kept
            if ins.sem is not None:
                snap = dict(kn)
                snap[id(ins.sem)] = ins.val
                vc[id(ins)] = snap
        self.known_final = known

        def run(engname, eng):
            for ins in lists[engname]:
                wl_ = list(ins.waits)
                emb = None
                if wl_ and (EMBED_DMA or not ins.dma):
                    emb = wl_.pop(0)
                for (sm_, vl_) in reversed(wl_):
                    eng.wait_ge(sm_, vl_)
                    nwaits[engname] += 1
                r = ins.fn(eng)
                if emb is not None:
                    r._wait_ge(emb[0], emb[1])
                if ins.sem is not None:
                    r.then_inc(ins.sem, 16 if ins.dma else 1)
            if engname == "sp":
                kn = known["sp"]
                for d in fin:
                    if kn.get(id(d.sem), 0) < d.val:
                        eng.wait_ge(d.sem, d.val)

        with nc.Block() as block:
            @block.tensor
            def _(e):
                run("pe", e)

            @block.scalar
            def _(e):
                run("act", e)

            @block.vector
            def _(e):
                run("dve", e)

            @block.gpsimd
            def _(e):
                run("pool", e)

            @block.sync
            def _(e):
                run("sp", e)
        self.nwaits = nwaits


class Arena:
    def __init__(self, t, words):
        self.t = t
        self.words = words
        self.off = 0

    def reset(self):
        self.off = 0

    def alloc(self, parts, shape, dt):
        n = 1
        for s in shape:
            n *= s
        nbytes = n * (2 if dt == BF16 else 4)
        w = (nbytes + 3) // 4
        w = (w + 15) // 16 * 16
        assert self.off + w <= self.words, ("arena overflow", self.off, w, self.words)
        v = self.t[:, self.off:self.off + w]
        self.off += w
        if dt == BF16:
            v = v.bitcast(BF16)
        v = v[0:parts, 0:n]
        if len(shape) == 2:
            v = v.rearrange("p (a b) -> p a b", a=shape[0])
        elif len(shape) == 3:
            v = v.rearrange("p (a b c) -> p a b c", a=shape[0], b=shape[1])
        return v


def bc_inner(v, n):
    return bass.AP(v.tensor, v.offset, [list(d) for d in v.ap] + [[0, n]])


def bc_mid(v, n):
    a = [list(d) for d in v.ap]
    return bass.AP(v.tensor, v.offset, [a[0], [0, n]] + a[1:])


def _fsz(ap):
    n = 1
    for d in ap.shape[1:]:
        n *= d
    return n


def MM(P, out, lhsT, rhs, start=True, stop=True, reads=(), excl=()):
    n = _fsz(rhs)
    c = max(0.056, n / 2400.0 + 0.004) * (3.0 if rhs.dtype == F32 else 1.0)
    P.emit("pe", lambda e: e.matmul(out, lhsT=lhsT, rhs=rhs, start=start, stop=stop), reads=reads, excl=excl, cost=c)


def TRN(P, out, in_, ident, reads=(), excl=()):
    P.emit("pe", lambda e: e.transpose(out=out, in_=in_, identity=ident), reads=reads, excl=excl, cost=0.07)


def ACTV(P, out, in_, func, reads=(), writes=(), excl=(), **kw):
    P.emit("act", lambda e: e.activation(out=out, in_=in_, func=func, **kw), reads=reads, writes=writes, excl=excl,
           cost=0.25 + _fsz(out) / 1200.0)


def CPY(P, eng, out, in_, reads=(), writes=(), excl=()):
    if eng == "act":
        P.emit("act", lambda e: e.copy(out=out, in_=in_), reads=reads, writes=writes, excl=excl, cost=0.25 + _fsz(out) / 1200.0)
    else:
        P.emit(eng, lambda e: e.tensor_copy(out=out, in_=in_), reads=reads, writes=writes, excl=excl, cost=_vc(eng, out))


def _vc(eng, out):
    n = _fsz(out)
    if eng == "pool":
        return 0.15 + n / 480.0
    return 0.16 + n / 960.0


def TT(P, eng, out, in0, in1, op, reads=(), writes=(), excl=()):
    P.emit(eng, lambda e: e.tensor_tensor(out=out, in0=in0, in1=in1, op=op), reads=reads, writes=writes, excl=excl, cost=_vc(eng, out))


def TS(P, eng, out, in0, s1, s2, op0, op1=None, reads=(), writes=(), excl=()):
    if op1 is None:
        P.emit(eng, lambda e: e.tensor_scalar(out=out, in0=in0, scalar1=s1, scalar2=None, op0=op0), reads=reads, writes=writes, excl=excl, cost=_vc(eng, out))
    else:
        P.emit(eng, lambda e: e.tensor_scalar(out=out, in0=in0, scalar1=s1, scalar2=s2, op0=op0, op1=op1), reads=reads, writes=writes, excl=excl, cost=_vc(eng, out))


def STT(P, eng, out, in0, scalar, in1, op0, op1, reads=(), writes=(), excl=()):
    P.emit(eng, lambda e: e.scalar_tensor_tensor(out=out, in0=in0, scalar=scalar, in1=in1, op0=op0, op1=op1),
           reads=reads, writes=writes, excl=excl, cost=_vc(eng, out))


def RCP(P, out, in_, reads=(), writes=()):
    P.emit("dve", lambda e: e.reciprocal(out=out, in_=in_), reads=reads, writes=writes, cost=_vc("dve", out))


def DMA(P, eng, out, in_, reads=(), writes=()):
    nb = 4 * 128 * _fsz(out if len(out.shape) >= len(in_.shape) else in_)
    P.emit(eng, lambda e: e.dma_start(out=out, in_=in_), reads=reads, writes=writes, dma=True, cost=4.0 + nb / 80e3)


def MSET(P, ap, val, writes=()):
    P.emit("pool", lambda e: e.memset(ap, val), writes=writes, cost=0.15 + _fsz(ap) / 960.0)


class K:
    pass


def build_program(debug=False, stop_after=99):
    nc = bass.Bass("TRN2", target_bir_lowering=False)
    k = K()
    k.nc = nc
    k.debug = debug

    def din(name, shape):
        return nc.dram_tensor(name, list(shape), F32, kind="ExternalInput").ap()

    k.x = din("x", [T, D])
    k.norm1_w = din("norm1_w", [1, D])
    k.w_in = din("w_in", [D, IN_DIM])
    k.cwT = din("cwT", [128, 96])
    k.a_log = din("a_log", [1, 8])
    k.dt_bias = din("dt_bias", [1, 8])
    k.dn_norm_w = din("dn_norm_w", [128, 1])
    k.w_proj_a = din("w_proj_a", [D, D])
    k.w_proj_b = din("w_proj_b", [512, D])
    k.w_out = din("w_out", [D, D])
    k.norm2_w = din("norm2_w", [1, D])
    k.w_gate_up = din("w_gate_up", [D, 2 * DFF])
    k.w_down = din("w_down", [DFF, D])
    k.final_norm_w = din("final_norm_w", [1, D])
    k.c_ident = din("c_ident", [128, 128])
    k.c_maskb = din("c_maskb", [128, 256])
    k.c_cos = din("c_cos", [32, T])
    k.c_sin = din("c_sin", [32, T])
    k.c_pm = din("c_pm", [32, 32])
    k.c_tri = din("c_tri", [CH, CH])
    k.c_u = din("c_u", [CH, CH])
    k.out = nc.dram_tensor("out", [T, D], F32, kind="ExternalOutput").ap()
    k.oaT_d = nc.dram_tensor("oaT_d", [8, 128, T], BF16, kind="Internal").ap()
    k.obT_d = nc.dram_tensor("obT_d", [4, 128, T], BF16, kind="Internal").ap()
    k.x1_d = nc.dram_tensor("x1_d", [T, D], F32, kind="Internal").ap()
    if debug:
        k.dbg_hT = nc.dram_tensor("dbg_hT", [128, 8, T], BF16, kind="ExternalOutput").ap()
        k.dbg_oa = nc.dram_tensor("dbg_oa", [8, 128, T], BF16, kind="ExternalOutput").ap()
        k.dbg_ob = nc.dram_tensor("dbg_ob", [4, 128, T], BF16, kind="ExternalOutput").ap()
        k.dbg_x1 = nc.dram_tensor("dbg_x1", [T, D], F32, kind="ExternalOutput").ap()

    with contextlib.ExitStack() as st:
        P = Prog(nc)
        k.P = P
        sb = lambda name, shape, dt: st.enter_context(nc.sbuf_tensor(name, shape, dt))
        ARW = 50 * 1024
        k.arena_t = sb("arena", [128, ARW], F32)
        k.A = Arena(k.arena_t, ARW)
        k.hT = k.A.alloc(128, (8, T), BF16)
        k.hT_end = k.A.off
        k.ident_f = sb("ident_f", [128, 128], F32)
        k.ident_b = sb("ident_b", [128, 128], BF16)
        k.ones_b = sb("ones_b", [128, 128], BF16)
        k.ones_f = sb("ones_f", [128, 128], F32)
        k.nw_bc = sb("nw_bc", [128, D], F32)
        k.stat = sb("stat", [128, 64], F32)
        k.stat2 = sb("stat2", [128, 64], F32)
        k.psf = [st.enter_context(nc.psum_tensor("psf%d" % i, [128, 512], F32)) for i in range(8)]
        k.psb = [k.psf[6][:, :].bitcast(BF16), k.psf[7][:, :].bitcast(BF16)]

        DMA(P, "sp", k.ident_f[:], k.c_ident, writes=["ident_f"])
        DMA(P, "pool", k.ident_b[:], k.c_ident, writes=["ident_b"])
        MSET(P, k.ones_b[:], 1.0, writes=["ones_b"])
        MSET(P, k.ones_f[:], 1.0, writes=["ones_f"])

        hTk = ["hT%d" % i for i in range(32)]
        phase1(k, k.x, k.norm1_w, lambda tt: k.hT[:, :, tt * 128:(tt + 1) * 128], hTk, 32, 0)
        if debug:
            DMA(P, "sp", k.dbg_hT, k.hT, reads=hTk)
        if stop_after >= 2:
            P.barrier()
            k.A.off = k.hT_end
            phase2a(k)
        if stop_after >= 3:
            P.barrier()
            k.A.off = k.hT_end
            phase2b(k)
        if stop_after >= 4:
            P.barrier()
            k.A.off = k.hT_end
            phase3a(k)
        if stop_after >= 5:
            P.barrier()
            k.A.off = 0
            phase3b(k)
        P.finalize(st)
        k.counts = (P.counts, P.nwaits, {e: len(v) for e, v in P.lists.items()})
    return nc, k


def phase1(k, src, nw, dst_fn, dst_keys, ntiles, tile0, tag="p1", bufs=None):
    P, A = k.P, k.A
    NB = 12
    if bufs is None:
        bufs = ([A.alloc(128, (D,), F32) for _ in range(NB)], [A.alloc(128, (D,), BF16) for _ in range(NB)],
                A.alloc(128, (D,), BF16))
    xt, xn, junk = bufs
    nwk = tag + "nw"
    DMA(P, "sp", k.nw_bc[:], bass.AP(nw.tensor, nw.offset, [[0, 128], [1, D]]), writes=[nwk])
    st = k.stat
    for i in range(ntiles):
        s = i % NB
        pbi = i % 2
        tt = tile0 + i
        xk, nk = "%sxt%d" % (tag, s), "%sxn%d" % (tag, s)
        sk = "%sst%d" % (tag, s)
        DMA(P, "sp", xt[s], src[tt * 128:(tt + 1) * 128, :], writes=[xk])
        ACTV(P, junk, xt[s], AF.Square, reads=[xk], writes=[tag + "junk", sk], accum_out=st[:, s:s + 1])
        ACTV(P, st[:, 16 + s:17 + s], st[:, s:s + 1], AF.Sqrt, reads=[sk], writes=[sk + "b"], scale=1.0 / D, bias=EPS)
        RCP(P, st[:, 32 + s:33 + s], st[:, 16 + s:17 + s], reads=[sk + "b"], writes=[sk + "c"])
        STT(P, "dve", xn[s], xt[s], st[:, 32 + s:33 + s], k.nw_bc[:], ALU.mult, ALU.mult, reads=[xk, sk + "c", nwk], writes=[nk])
        pb = k.psb[pbi]
        for kc in range(8):
            TRN(P, pb[:, kc * 128:(kc + 1) * 128], xn[s][:, kc * 128:(kc + 1) * 128], k.ident_b[:],
                reads=[nk, "ident_b"], excl=["psb%d" % pbi])
        CPY(P, "dve" if i % 2 else "act", dst_fn(i), pb[:].rearrange("p (a b) -> p a b", a=8), writes=[dst_keys[i]], excl=["psb%d" % pbi])
    return bufs


def wload(k, dst, src_rows, col0, ncols, nkc, key, row0=0):
    src = src_rows[row0:row0 + nkc * 128, col0:col0 + ncols].rearrange("(kc p) c -> p kc c", p=128)
    DMA(k.P, "pool", dst, src, writes=[key])


def f2(v):
    return v.rearrange("p a b -> p (a b)")


def run_streams(gens, lead=0):
    active = list(gens)
    for _ in range(lead):
        try:
            next(active[0])
        except StopIteration:
            active.pop(0)
            break
    while active:
        for g_ in list(active):
            try:
                next(g_)
            except StopIteration:
                active.remove(g_)


def phase2a(k):
    P, A = k.P, k.A
    psf = k.psf
    hTk = ["hT%d" % i for i in range(32)]
    hT = k.hT

    def hkeys(t0, n):
        return hTk[t0 // 128:(t0 + n + 127) // 128]

    C = CH
    NCB = 512 // C
    NCH = T // C
    NLV = 6 if C == 128 else 5
    tri = A.alloc(C, (C,), F32)
    um = A.alloc(C, (C,), F32)
    cw = A.alloc(128, (96,), F32)
    dnwh = A.alloc(128, (1,), F32)
    alog = A.alloc(C, (8,), F32)
    dtb = A.alloc(C, (8,), F32)
    nA = A.alloc(C, (8,), F32)
    mhalf = A.alloc(128, (512,), F32)
    DMA(P, "sp", tri, k.c_tri, writes=["tri"])
    DMA(P, "sp", um, k.c_u, writes=["um"])
    DMA(P, "sp", cw, k.cwT, writes=["cw"])
    DMA(P, "sp", dnwh, k.dn_norm_w, writes=["dnw0"])
    TS(P, "dve", dnwh, dnwh, 1.0, None, ALU.mult, reads=["dnw0"], writes=["dnwh"])
    DMA(P, "sp", alog, bass.AP(k.a_log.tensor, 0, [[0, C], [1, 8]]), writes=["alog"])
    DMA(P, "sp", dtb, bass.AP(k.dt_bias.tensor, 0, [[0, C], [1, 8]]), writes=["dtb"])
    MSET(P, mhalf, -0.5, writes=["mhalf"])
    ACTV(P, nA, alog, AF.Exp, reads=["alog"], writes=["nA0"])
    TS(P, "dve", nA, nA, -1.0, None, ALU.mult, reads=["nA0"], writes=["nA"])
    hbeta = A.alloc(C, (8, NCH), F32)
    nbeta = A.alloc(C, (8, NCH), F32)
    g = A.alloc(C, (8, NCH), F32)
    egc = A.alloc(C, (8, NCH), F32)
    edl = A.alloc(C, (8, NCH), F32)
    begc = A.alloc(C, (8, NCH), F32)
    egl = A.alloc(128, (8, NCH), F32)
    shared_end = A.off
    wbd = A.alloc(128, (8, 16), BF16)
    bd = A.alloc(C, (NCH, 16), F32)
    beta = A.alloc(C, (8, NCH), F32)
    gc = A.alloc(C, (8, NCH), F32)
    glast = A.alloc(128, (8, NCH), F32)
    tmpg = A.alloc(C, (8, NCH), F32)
    wload(k, wbd, k.w_in, OFF["db"], 16, 8, "wbd")
    for n in range(NCH):
        bank = n // 32
        for kc in range(8):
            MM(P, psf[bank][0:C, (n % 32) * 16:(n % 32) * 16 + 16], hT[:, kc, n * C:(n + 1) * C], wbd[:, kc, :],
               start=(kc == 0), stop=(kc == 7), reads=["wbd"] + hkeys(n * C, C), excl=["psf%d" % bank])
    for bank in range((NCH + 31) // 32):
        CPY(P, "act", bd[:, bank * 32:(bank + 1) * 32, :], psf[bank][0:C, :].rearrange("p (a b) -> p a b", a=32),
            writes=["bd%d" % bank], excl=["psf%d" % bank])
    if NCH <= 32:
        CPY(P, "act", bd[:, 0:1, 0:1], bd[:, 0:1, 0:1], reads=["bd0"], writes=["bd1"])
    bdk = ["bd0", "bd1"]
    bd_b = bd[:, :, 0:8].rearrange("p n h -> p h n")
    bd_a = bd[:, :, 8:16].rearrange("p n h -> p h n")
    ACTV(P, beta, bd_b, AF.Exp, reads=bdk, writes=["beta0"], scale=-1.0)
    ACTV(P, beta, beta, AF.Ln, reads=["beta0"], writes=["beta0"], bias=1.0, scale=1.0)
    ACTV(P, beta, beta, AF.Exp, reads=["beta0"], writes=["beta"], scale=-1.0)
    TS(P, "dve", nbeta, beta, -1.0, None, ALU.mult, reads=["beta"], writes=["nbeta"])
    TS(P, "dve", hbeta, beta, 1.0, None, ALU.mult, reads=["beta"], writes=["hbeta"])
    TT(P, "dve", tmpg, bd_a, bc_inner(dtb, NCH), ALU.add, reads=bdk + ["dtb"], writes=["tmpg"])
    ACTV(P, tmpg, tmpg, AF.Exp, reads=["tmpg"], writes=["tmpg"])
    ACTV(P, tmpg, tmpg, AF.Ln, reads=["tmpg"], writes=["tmpg"], bias=1.0, scale=1.0)
    TT(P, "dve", g, tmpg, bc_inner(nA, NCH), ALU.mult, reads=["tmpg", "nA"], writes=["g"])
    g2 = g.rearrange("p h n -> p (h n)")
    GW = 8 * NCH
    MM(P, psf[2][0:C, 0:GW], tri, g2, reads=["tri", "g"], excl=["psf2"])
    MM(P, psf[3][:, 0:GW], k.ones_f[0:C, :], g2, reads=["ones_f", "g"], excl=["psf3"])
    CPY(P, "act", f2(gc), psf[2][0:C, 0:GW], writes=["gc"], excl=["psf2"])
    ACTV(P, f2(egc), psf[2][0:C, 0:GW], AF.Exp, writes=["egc"], excl=["psf2"])
    CPY(P, "act", f2(glast), psf[3][:, 0:GW], writes=["glast"], excl=["psf3"])
    ACTV(P, f2(egl), psf[3][:, 0:GW], AF.Exp, writes=["egl"], excl=["psf3"])
    TT(P, "dve", edl, glast[0:C], gc, ALU.subtract, reads=["glast", "gc"], writes=["edl0"])
    ACTV(P, edl, edl, AF.Exp, reads=["edl0"], writes=["edl"])
    TT(P, "pool", begc, beta, egc, ALU.mult, reads=["beta", "egc"], writes=["begc"])
    P.barrier()
    A.off = shared_end
    shared = ["tri", "um", "cw", "dnwh", "mhalf", "hbeta", "nbeta", "g", "egc", "edl", "begc", "egl"]
    names = ["dq", "dk", "dv", "dz"]
    id64b = k.ident_b[0:C, 0:C]
    id64f = k.ident_f[0:C, 0:C]
    EPS4 = EPS

    def stream(sid, heads):
        K_ = lambda name: "s%d_%s" % (sid, name)
        wq = [A.alloc(128, (8, 128), BF16) for _ in range(4)]
        dg = A.alloc(128, (12, 128), BF16)
        raw = [[A.alloc(128, (515,), BF16) for _ in range(2)] for _ in range(3)]
        qs = A.alloc(128, (512,), F32)
        ks = A.alloc(128, (512,), F32)
        vs2 = [A.alloc(128, (512,), BF16) for _ in range(2)]
        zs2 = [A.alloc(128, (512,), BF16) for _ in range(2)]
        sq = A.alloc(128, (512,), BF16)
        rn = A.alloc(128, (512,), F32)
        oTb = A.alloc(128, (512,), F32)
        qTb2 = [A.alloc(128, (512,), BF16) for _ in range(2)]
        kTb2 = [A.alloc(128, (512,), BF16) for _ in range(2)]
        qdT2 = [A.alloc(128, (512,), BF16) for _ in range(2)]
        kbg = A.alloc(C, (NCB, 128), BF16)
        kd2 = [A.alloc(C, (NCB, 128), BF16) for _ in range(2)]
        vb = A.alloc(C, (NCB, 128), BF16)
        GU = A.alloc(C, (NCB, C), F32)
        GT = A.alloc(C, (NCB, C), F32)
        Pm = [A.alloc(C, (NCB, C), BF16) for _ in range(2)]
        Qm = [A.alloc(C, (NCB, C), BF16) for _ in range(2)]
        Xm = [A.alloc(C, (NCB, C), BF16) for _ in range(2)]
        dexp = A.alloc(C, (NCB, C), BF16)
        u2 = [A.alloc(C, (NCB, 128), BF16) for _ in range(2)]
        wT2 = [A.alloc(128, (512,), BF16) for _ in range(2)]
        qkT2 = [A.alloc(C, (NCB, C), BF16) for _ in range(2)]
        S_f = A.alloc(128, (128,), F32)
        S_b = A.alloc(128, (128,), BF16)
        vnew = [A.alloc(C, (128,), BF16) for _ in range(2)]
        ob = [A.alloc(128, (512,), BF16) for _ in range(2)]
        tt_ = rn
        Dm, DTm = GU, GT
        base = 4 * sid
        G = [psf[base], psf[base + 1]]
        Gk = [["psf%d" % base], ["psf%d" % (base + 1)]]
        R = [psf[base + 2], psf[base + 3]]
        Rk = [["psf%d" % (base + 2)], ["psf%d" % (base + 3)]]
        prX = [0]
        prY = [0]

        def nextpsX():
            i = prX[0] % 2
            prX[0] += 1
            return R[i], Rk[i]

        def nextpsY():
            i = prY[0] % 2
            prY[0] += 1
            return G[i], Gk[i]

        def silu(dst, dkey, ps, pk_):
            ACTV(P, tt_, ps[:, :], AF.Exp, writes=[K_("rn")], excl=pk_, scale=-1.0)
            ACTV(P, tt_, tt_, AF.Ln, reads=[K_("rn")], writes=[K_("rn")], bias=1.0, scale=1.0)
            ACTV(P, tt_, tt_, AF.Exp, reads=[K_("rn")], writes=[K_("rn")], scale=-1.0)
            TT(P, "dve", dst, tt_, ps[:, :], ALU.mult, reads=[K_("rn")], writes=[dkey], excl=pk_)

        def stageA1(h, b, ui):
            pz = ui % 2
            t0 = b * 512
            rb = b % 2
            hk = hkeys(t0, 512)
            zs, vs, qTb, kTb = zs2[pz], vs2[pz], qTb2[pz], kTb2[pz]
            if b == 0:
                for ti in range(4):
                    wload(k, wq[ti], k.w_in, OFF[names[ti]] + h * 128, 128, 8, K_("wq%d" % ti))
                for ti in range(3):
                    for j in range(4):
                        col = ti * 32 + h * 4 + j
                        TS(P, "dve", dg[:, ti * 4 + j, :], k.ident_f[:], cw[:, col:col + 1], None, ALU.mult,
                           reads=["ident_f", "cw"], writes=[K_("dg%d" % (ti * 4 + j))])
                for ti in range(3):
                    MSET(P, raw[ti][0][:, 0:3], 0.0, writes=[K_("rawh%d_0" % ti)])
                yield
            for ti in range(4):
                ps, pk_ = nextpsX()
                for kc in range(8):
                    MM(P, ps[:, :], wq[ti][:, kc, :], hT[:, kc, t0:t0 + 512], start=(kc == 0), stop=(kc == 7),
                       reads=[K_("wq%d" % ti)] + hk, excl=pk_)
                if ti < 3:
                    CPY(P, "act", raw[ti][rb][:, 3:515], ps[:, :], writes=[K_("raw%d_%d" % (ti, rb))], excl=pk_)
                    CPY(P, "act", raw[ti][1 - rb][:, 0:3], ps[:, 509:512], writes=[K_("rawh%d_%d" % (ti, 1 - rb))], excl=pk_)
                else:
                    silu(zs, K_("zs%d" % pz), ps, pk_)
                yield
            for ti in range(3):
                ps, pk_ = nextpsX()
                for j in range(4):
                    MM(P, ps[:, :], dg[:, ti * 4 + j, :], raw[ti][rb][:, j:j + 512], start=(j == 0), stop=(j == 3),
                       reads=[K_("dg%d" % (ti * 4 + j)), K_("raw%d_%d" % (ti, rb)), K_("rawh%d_%d" % (ti, rb))], excl=pk_)
                silu((qs, ks, vs)[ti], K_(("qs", "ks", "vs%d" % pz)[ti]), ps, pk_)
                yield
            for ti in range(2):
                src = (qs, ks)[ti]
                sk = K_(("qs", "ks")[ti])
                ps, pk_ = nextpsX()
                TT(P, "pool", sq, src, src, ALU.mult, reads=[sk], writes=[K_("sq")])
                MM(P, ps[:, :], k.ones_b[:], sq, reads=["ones_b", K_("sq")], excl=pk_)
                ACTV(P, rn, ps[:, :], AF.Ln, writes=[K_("rn")], excl=pk_, bias=EPS, scale=1.0)
                ACTV(P, rn, rn, AF.Exp, reads=[K_("rn")], writes=[K_("rn")], scale=-0.5)
                if ti == 0:
                    STT(P, "dve", qTb, qs, 128.0 ** -0.5, rn, ALU.mult, ALU.mult, reads=[sk, K_("rn")], writes=[K_("qTb%d" % pz)])
                else:
                    TT(P, "dve", kTb, ks, rn, ALU.mult, reads=[sk, K_("rn")], writes=[K_("kTb%d" % pz)])
                yield

        def stageA2(h, b, ui):
            pz = ui % 2
            nextps = nextpsY
            vs, qTb, kTb = vs2[pz], qTb2[pz], kTb2[pz]
            kTk, qTk, vsk = K_("kTb%d" % pz), K_("qTb%d" % pz), K_("vs%d" % pz)
            kd, u, wT, qkT, qdT = kd2[pz], u2[pz], wT2[pz], qkT2[pz], qdT2[pz]
            cs = slice(b * NCB, (b + 1) * NCB)
            ps, pk_ = nextps()
            psk = ps[:, :].bitcast(BF16)
            for c in range(NCB):
                TRN(P, psk[0:C, c * 128:(c + 1) * 128], kTb[:, c * C:(c + 1) * C], k.ident_b[:], reads=[kTk, "ident_b"], excl=pk_)
            pk = psk[0:C, 0:NCB * 128].rearrange("p (a b) -> p a b", a=NCB)
            TT(P, "dve", kbg, pk, bc_inner(begc[:, h, cs], 128), ALU.mult, reads=["begc"], writes=[K_("kbg")], excl=pk_)
            TT(P, "dve", kd, pk, bc_inner(edl[:, h, cs], 128), ALU.mult, reads=["edl"], writes=[K_("kd%d" % pz)], excl=pk_)
            yield
            ps, pk_ = nextps()
            psv = ps[:, :].bitcast(BF16)
            for c in range(NCB):
                TRN(P, psv[0:C, c * 128:(c + 1) * 128], vs[:, c * C:(c + 1) * C], k.ident_b[:], reads=[vsk, "ident_b"], excl=pk_)
            pv = psv[0:C, 0:NCB * 128].rearrange("p (a b) -> p a b", a=NCB)
            TT(P, "dve", vb, pv, bc_inner(hbeta[:, h, cs], 128), ALU.mult, reads=["hbeta"], writes=[K_("vb")], excl=pk_)
            yield
            gh = g[:, h, cs]
            TT(P, "pool", GU, bc_mid(um, NCB), bc_inner(gh, C), ALU.mult, reads=["um", "g"], writes=[K_("GU")])
            TT(P, "pool", GT, bc_mid(tri, NCB), bc_inner(gh, C), ALU.mult, reads=["tri", "g"], writes=[K_("GT")])
            ps, pk_ = nextps()
            MM(P, ps[0:C, :], tri, f2(GU), reads=["tri", K_("GU")], excl=pk_)
            ACTV(P, f2(Dm), ps[0:C, :], AF.Exp, reads=[K_("GU")], writes=[K_("GU")], excl=pk_)
            yield
            ps, pk_ = nextps()
            MM(P, ps[0:C, :], um, f2(GT), reads=["um", K_("GT")], excl=pk_)
            ACTV(P, f2(DTm), ps[0:C, :], AF.Exp, reads=[K_("GT")], writes=[K_("GT")], excl=pk_)
            TT(P, "pool", Dm, Dm, bc_mid(um, NCB), ALU.mult, reads=[K_("GU"), "um"], writes=[K_("GU")])
            TT(P, "pool", Dm, Dm, bc_inner(nbeta[:, h, cs], C), ALU.mult, reads=[K_("GU"), "nbeta"], writes=[K_("GU")])
            TT(P, "pool", DTm, DTm, bc_mid(tri, NCB), ALU.mult, reads=[K_("GT"), "tri"], writes=[K_("GT")])
            yield
            ps, pk_ = nextps()
            for c in range(NCB):
                MM(P, ps[0:C, c * C:(c + 1) * C], kTb[:, c * C:(c + 1) * C], kTb[:, c * C:(c + 1) * C], reads=[kTk], excl=pk_)
            TT(P, "dve", f2(Qm[0]), ps[0:C, :], f2(Dm), ALU.mult, reads=[K_("GU")], writes=[K_("Q0")], excl=pk_)
            yield
            ps, pk_ = nextps()
            psq = ps[:, :].bitcast(BF16)
            for c in range(NCB):
                TRN(P, psq[0:C, c * C:(c + 1) * C], Qm[0][:, c, :], id64b, reads=[K_("Q0"), "ident_b"], excl=pk_)
            CPY(P, "act", f2(Pm[0]), psq[0:C, 0:512], writes=[K_("P0")], excl=pk_)
            TT(P, "pool", Xm[0], Pm[0], bc_mid(id64b, NCB), ALU.add, reads=[K_("P0"), "ident_b"], writes=[K_("X0")])
            yield

            def PQ(lv):
                a, bcur = (lv - 1) % 2, lv % 2
                Qa, Pa = K_("Q%d" % a), K_("P%d" % a)
                Qb, Pb = K_("Q%d" % bcur), K_("P%d" % bcur)
                if lv < NLV:
                    ps, pk_ = nextps()
                    for c in range(NCB):
                        MM(P, ps[0:C, c * C:(c + 1) * C], Qm[a][:, c, :], Pm[a][:, c, :], reads=[Qa, Pa], excl=pk_)
                    CPY(P, "act", f2(Pm[bcur]), ps[0:C, :], writes=[Pb], excl=pk_)
                ps, pk_ = nextps()
                for c in range(NCB):
                    MM(P, ps[0:C, c * C:(c + 1) * C], Pm[a][:, c, :], Qm[a][:, c, :], reads=[Qa, Pa], excl=pk_)
                CPY(P, "act", f2(Qm[bcur]), ps[0:C, :], writes=[Qb], excl=pk_)

            def XU(lv):
                a, bcur = (lv - 1) % 2, lv % 2
                Qb, Xa, Xb = K_("Q%d" % bcur), K_("X%d" % a), K_("X%d" % bcur)
                ps, pk_ = nextps()
                for c in range(NCB):
                    MM(P, ps[0:C, c * C:(c + 1) * C], Qm[bcur][:, c, :], Xm[a][:, c, :], reads=[Qb, Xa], excl=pk_)
                TT(P, "dve", f2(Xm[bcur]), ps[0:C, :], f2(Xm[a]), ALU.add, reads=[Xa], writes=[Xb], excl=pk_)

            PQ(1)
            yield
            for lv in range(2, NLV + 1):
                PQ(lv)
                yield
                XU(lv - 1)
                yield
            XU(NLV)
            yield
            X = Xm[NLV % 2]
            Xk = K_("X%d" % (NLV % 2))
            for half in range(NCB // 4):
                ps, pk_ = nextps()
                for c in range(4):
                    cc = half * 4 + c
                    MM(P, ps[0:C, c * 128:(c + 1) * 128], X[:, cc, :], vb[:, cc, :], reads=[Xk, K_("vb")], excl=pk_)
                CPY(P, "dve", f2(u[:, half * 4:(half + 1) * 4, :]), ps[0:C, :], writes=[K_("u%d_%d" % (pz, half))], excl=pk_)
                yield
            ps, pk_ = nextps()
            for c in range(NCB):
                MM(P, ps[:, c * C:(c + 1) * C], kbg[:, c, :], X[:, c, :], reads=[Xk, K_("kbg")], excl=pk_)
            CPY(P, "act", wT, ps[:, :], writes=[K_("wT%d" % pz)], excl=pk_)
            yield
            ps, pk_ = nextps()
            for c in range(NCB):
                MM(P, ps[0:C, c * C:(c + 1) * C], kTb[:, c * C:(c + 1) * C], qTb[:, c * C:(c + 1) * C],
                   reads=[kTk, qTk], excl=pk_)
            TT(P, "dve", f2(qkT), ps[0:C, :], f2(DTm), ALU.mult, reads=[K_("GT")], writes=[K_("qkT%d" % pz)], excl=pk_)
            yield
            TT(P, "pool", dexp, bc_mid(id64f, NCB), bc_inner(egc[:, h, cs], C), ALU.mult, reads=["ident_f", "egc"], writes=[K_("dexp")])
            ps, pk_ = nextps()
            MM(P, ps[:, :], k.ones_b[0:C, :], f2(dexp), reads=["ones_b", K_("dexp")], excl=pk_)
            TT(P, "dve", qdT, ps[:, :], qTb, ALU.mult, reads=[qTk], writes=[K_("qdT%d" % pz)], excl=pk_)
            yield

        def stageB(h, b, ui):
            pz = ui % 2
            t0 = b * 512
            zs = zs2[pz]
            kd, u, wT, qkT, qdT = kd2[pz], u2[pz], wT2[pz], qkT2[pz], qdT2[pz]
            if b == 0:
                MSET(P, S_f, 0.0, writes=[K_("S_f")])
                MSET(P, S_b, 0.0, writes=[K_("S_b")])
            for c in range(NCB):
                n = b * NCB + c
                Rb_, Rk_ = R[c % 2], Rk[c % 2]
                vn = vnew[c % 2]
                vk = K_("vnew%d" % (c % 2))
                c64 = slice(c * C, (c + 1) * C)
                oc = slice(256 + (c // 2) * C, 256 + (c // 2) * C + C)
                MM(P, Rb_[0:C, 0:128], wT[:, c64], S_b, reads=[K_("wT%d" % pz), K_("S_b")], excl=Rk_)
                TT(P, "dve", vn, u[:, c, :], Rb_[0:C, 0:128], ALU.subtract, reads=[K_("u%d_%d" % (pz, c // 4))], writes=[vk], excl=Rk_)
                yield
                MM(P, Rb_[:, oc], S_b, qdT[:, c64], start=True, stop=False, reads=[K_("S_b"), K_("qdT%d" % pz)], excl=Rk_)
                MM(P, Rb_[:, oc], vn, qkT[:, c, :], start=False, stop=True, reads=[vk, K_("qkT%d" % pz)], excl=Rk_)
                MM(P, Rb_[:, 128:256], kd[:, c, :], vn, reads=[vk, K_("kd%d" % pz)], excl=Rk_)
                STT(P, "dve", S_b, S_f, egl[:, h, n:n + 1], Rb_[:, 128:256], ALU.mult, ALU.add,
                    reads=[K_("S_f"), "egl"], writes=[K_("S_b")], excl=Rk_)
                STT(P, "dve", S_f, S_f, egl[:, h, n:n + 1], Rb_[:, 128:256], ALU.mult, ALU.add,
                    reads=[K_("S_f"), "egl"], writes=[K_("S_f")], excl=Rk_)
                yield
            o4 = oTb.rearrange("p (a t c) -> p a t c", a=NCB // 2, t=2)
            for par in range(2):
                CPY(P, "act", o4[:, :, par, :], R[par][:, 256:512].rearrange("p (a c) -> p a c", a=NCB // 2),
                    reads=[K_("oTb")], writes=[K_("oTb")], excl=Rk[par])
            yield
            ps, pk_ = nextpsX()
            TT(P, "pool", sq, oTb, oTb, ALU.mult, reads=[K_("oTb")], writes=[K_("sq")])
            MM(P, ps[:, :], k.ones_b[:], sq, reads=["ones_b", K_("sq")], excl=pk_)
            ACTV(P, rn, ps[:, :], AF.Ln, writes=[K_("rn")], excl=pk_, bias=EPS, scale=1.0 / 128)
            ACTV(P, rn, rn, AF.Exp, reads=[K_("rn")], writes=[K_("rn")], scale=-0.5)
            yield
            TT(P, "dve", oTb, oTb, rn, ALU.mult, reads=[K_("oTb"), K_("rn")], writes=[K_("oTb")])
            o_ = ob[b % 2]
            okey = K_("ob%d" % (b % 2))
            STT(P, "dve", o_, oTb, dnwh[:, 0:1], zs, ALU.mult, ALU.mult, reads=[K_("oTb"), "dnwh", K_("zs%d" % pz)], writes=[okey])
            DMA(P, "sp", k.oaT_d[h, :, t0:t0 + 512], o_, reads=[okey], writes=["oaT_d"])
            if k.debug:
                DMA(P, "sp", k.dbg_oa[h, :, t0:t0 + 512], o_, reads=[okey])
            yield

        def seq(*gens):
            for gg in gens:
                yield from gg

        def merge(g1_, g2_):
            act_ = [g1_, g2_]
            while act_:
                for gg in list(act_):
                    try:
                        next(gg)
                        yield
                    except StopIteration:
                        act_.remove(gg)

        units = [(h, b) for h in heads for b in range(8)]
        NU = len(units)
        yield from stageA1(units[0][0], units[0][1], 0)
        yield from stageA1(units[1][0], units[1][1], 1)
        yield from stageA2(units[0][0], units[0][1], 0)
        for ui in range(NU):
            xs_ = [stageB(units[ui][0], units[ui][1], ui)]
            if ui + 2 < NU:
                xs_.append(stageA1(units[ui + 2][0], units[ui + 2][1], ui + 2))
            if ui + 1 < NU:
                yield from merge(seq(*xs_), stageA2(units[ui + 1][0], units[ui + 1][1], ui + 1))
            else:
                yield from seq(*xs_)

    g0 = stream(0, [0, 2, 4, 6])
    g1 = stream(1, [1, 3, 5, 7])
    run_streams([g0, g1], lead=20)


def phase2b(k):
    P, A = k.P, k.A
    psf = k.psf
    hT = k.hT
    hTk = ["hT%d" % i for i in range(32)]
    cos = A.alloc(32, (T,), F32)
    sin = A.alloc(32, (T,), F32)
    maskb = A.alloc(128, (256,), BF16)
    pmat = A.alloc(32, (32,), BF16)
    DMA(P, "sp", cos, k.c_cos, writes=["cos"])
    DMA(P, "sp", sin, k.c_sin, writes=["sin"])
    DMA(P, "pool", maskb, k.c_maskb, writes=["maskb"])
    DMA(P, "pool", pmat, k.c_pm, writes=["pmat"])
    w3 = [[A.alloc(128, (8, 128), BF16) for _ in range(3)] for _ in range(2)]
    qTp = [A.alloc(128, (T,), BF16) for _ in range(2)]
    kTp = [A.alloc(128, (T,), BF16) for _ in range(2)]
    vp = [A.alloc(128, (32, 128), BF16) for _ in range(2)]
    qraw = A.alloc(128, (512,), BF16)
    t1 = A.alloc(32, (512,), F32)
    t2 = A.alloc(32, (512,), F32)
    PT = [A.alloc(128, (256,), BF16) for _ in range(3)]
    num = A.alloc(128, (T,), F32)
    den = A.alloc(128, (T,), F32)
    obs = [A.alloc(128, (512,), BF16) for _ in range(2)]
    scale = 128.0 ** -0.5
    names = ["aq", "ak", "av"]
    heads = [(hs, gi) for hs in range(4) for gi in range(3)]

    def perm_view(buf, d, t0, p0, p1):
        m0 = t0 // d
        v = buf[p0:p1, :].rearrange("p (r m) -> p m r", r=d)
        return v[:, m0:m0 + 512 // d, :]

    def nat_view(buf, d, pos0):
        M = T // d
        if d == 1:
            return buf[:, pos0:pos0 + 512]
        if d == 4:
            r = pos0 // M
            m0 = pos0 % M
            return buf[:, :].rearrange("p (m r) -> p r m", r=4)[:, r, m0:m0 + 512]
        r0 = pos0 // M
        return buf[:, :].rearrange("p (m r) -> p r m", r=16)[:, r0:r0 + 2, :]

    def qkkeys(s):
        return ["%s%d%s%d" % (n_, s, s_, b) for n_ in ("qTp", "kTp") for s_ in ("b",) for b in range(8)]

    def proj_stage(i):
        hs, gi = heads[i]
        d = GROUPS[gi][1]
        hh = gi * 4 + hs
        s = i % 2
        nbr = (T // d) // 128
        w = w3[s]
        for ti in range(3):
            wload(k, w[ti], k.w_in, OFF[names[ti]] + hh * 128, 128, 8, "w3_%d_%d" % (s, ti))
        pend = None

        def rope(b, ti, ps, pk_):
            t0 = b * 512
            dstb = (qTp[s], kTp[s])[ti]
            dk_ = ("qTp%d" % s, "kTp%d" % s)[ti]
            dst = dstb[:, t0:t0 + 512]
            CPY(P, "act", dst, ps[:, :], writes=[dk_ + "b%d" % b], excl=pk_)
            MM(P, psf[2][0:32, :], pmat, dstb[0:32, t0:t0 + 512], reads=["pmat", dk_ + "b%d" % b], excl=["psf2"])
            TT(P, "dve", t1, ps[0:32, :], cos[:, t0:t0 + 512], ALU.mult, reads=["cos"], writes=["t1"], excl=pk_)
            TT(P, "dve", t2, psf[2][0:32, :], sin[:, t0:t0 + 512], ALU.mult, reads=["sin"], writes=["t2"], excl=["psf2"])
            TT(P, "pool", dstb[0:32, t0:t0 + 512], t1, t2, ALU.add, reads=["t1", "t2", dk_ + "b%d" % b], writes=[dk_ + "b%d" % b])

        u_ = 0
        for b in range(8):
            t0 = b * 512
            hk = hTk[b * 4:(b + 1) * 4]
            for ti in range(2):
                pi = u_ % 2
                u_ += 1
                ps = psf[pi]
                pk_ = ["psf%d" % pi]
                for kc in range(8):
                    MM(P, ps[:, :], w[ti][:, kc, :], hT[:, kc, t0:t0 + 512], start=(kc == 0), stop=(kc == 7),
                       reads=["w3_%d_%d" % (s, ti)] + hk, excl=pk_)
                if pend is not None:
                    rope(*pend)
                pend = (b, ti, ps, pk_)
                yield
        rope(*pend)
        yield
        for j in range(32):
            r = j // nbr
            n = j % nbr
            base = r + d * 128 * n
            for kc in range(8):
                hv = hT[:, kc, :]
                lh = bass.AP(hv.tensor, hv.offset + base, [list(hv.ap[0]), [d, 128]])
                MM(P, psf[3][:, (j % 4) * 128:(j % 4 + 1) * 128], lh, w[2][:, kc, :], start=(kc == 0), stop=(kc == 7),
                   reads=["w3_%d_2" % s] + hTk, excl=["psf3"])
            if j % 4 == 3:
                CPY(P, "act", f2(vp[s][:, j - 3:j + 1, :]), psf[3][:, :], writes=["vp%d_%d" % (s, j // 4)], excl=["psf3"])
            if j % 2 == 1:
                yield

    def core_stage(i):
        hs, gi = heads[i]
        d = GROUPS[gi][1]
        s = i % 2
        nbr = (T // d) // 128
        qk_keys = qkkeys(s)
        vkeys = ["vp%d_%d" % (s, x) for x in range(8)]
        q_, k_, v_ = qTp[s], kTp[s], vp[s]

        def front(j):
            r = j // nbr
            n = j % nbr
            W = 256 if n < nbr - 1 else 128
            pi = 4 + j % 2
            ps = psf[pi]
            pk_ = ["psf%d" % pi]
            pt = PT[j % 3]
            base = r + d * 128 * n
            kk_ = bass.AP(k_.tensor, k_.offset + base, [list(k_.ap[0]), [d, 128]])
            qq_ = bass.AP(q_.tensor, q_.offset + base, [list(q_.ap[0]), [d, W]])
            MM(P, ps[:, 0:W], kk_, qq_, start=True, stop=True, reads=qk_keys, excl=pk_)
            ACTV(P, pt[:, 0:W], ps[:, 0:W], AF.Exp, writes=["PT%d" % (j % 3)], excl=pk_, scale=scale)
            TT(P, "dve", pt[:, 0:W], pt[:, 0:W], maskb[:, 0:W], ALU.mult, reads=["PT%d" % (j % 3), "maskb"], writes=["PT%d" % (j % 3)])

        front(0)
        for j in range(32):
            if j + 1 < 32:
                front(j + 1)
            n = j % nbr
            pt = PT[j % 3]
            ptk = "PT%d" % (j % 3)
            col = (j % 4) * 128
            prevk = "PT%d" % ((j - 1) % 3)
            prev = PT[(j - 1) % 3]
            for which in range(2):
                pacc = 6 + which
                lh_prev = v_[:, max(j - 1, 0), :] if which == 0 else k.ones_b[:]
                lh_cur = v_[:, j, :] if which == 0 else k.ones_b[:]
                if n > 0:
                    MM(P, psf[pacc][:, col:col + 128], lh_prev, prev[:, 128:256], start=True, stop=False,
                       reads=vkeys + [prevk, "ones_b"], excl=["psf%d" % pacc])
                MM(P, psf[pacc][:, col:col + 128], lh_cur, pt[:, 0:128], start=(n == 0), stop=True,
                   reads=vkeys + [ptk, "ones_b"], excl=["psf%d" % pacc])
            if j % 4 == 3:
                pos0 = (j // 4) * 512
                nv = nat_view(num, d, pos0)
                dv_ = nat_view(den, d, pos0)
                if d == 16:
                    s6 = psf[6][:, :].rearrange("p (r m) -> p r m", r=2)
                    s7 = psf[7][:, :].rearrange("p (r m) -> p r m", r=2)
                else:
                    s6 = psf[6][:, :]
                    s7 = psf[7][:, :]
                if gi == 0:
                    CPY(P, "act", nv, s6, writes=["num"], excl=["psf6"])
                    CPY(P, "dve", dv_, s7, writes=["den"], excl=["psf7"])
                else:
                    TT(P, "dve", nv, nv, s6, ALU.add, reads=["num"], writes=["num"], excl=["psf6"])
                    TT(P, "dve", dv_, dv_, s7, ALU.add, reads=["den"], writes=["den"], excl=["psf7"])
            yield
        if gi == 2:
            RCP(P, den, den, reads=["den"], writes=["den"])
            for b in range(8):
                o_ = obs[b % 2]
                ok_ = "obs%d" % (b % 2)
                TT(P, "dve", o_, num[:, b * 512:(b + 1) * 512], den[:, b * 512:(b + 1) * 512], ALU.mult,
                   reads=["num", "den"], writes=[ok_])
                DMA(P, "sp", k.obT_d[hs, :, b * 512:(b + 1) * 512], o_, reads=[ok_], writes=["obT_d"])
                if k.debug:
                    DMA(P, "sp", k.dbg_ob[hs, :, b * 512:(b + 1) * 512], o_, reads=[ok_])
            yield

    run_streams([proj_stage(0)])
    for i in range(12):
        gens = [core_stage(i)]
        if i + 1 < 12:
            gens.append(proj_stage(i + 1))
        run_streams(gens)


def phase3a(k):
    P, A = k.P, k.A
    psf = k.psf
    hT = k.hT
    hTk = ["hT%d" % i for i in range(32)]
    wga = A.alloc(128, (8, D), BF16)
    wgb = A.alloc(128, (8, D), BF16)
    wpa = A.alloc(128, (8, D), BF16)
    wpb = A.alloc(128, (4, D), BF16)
    wo = A.alloc(128, (8, D), BF16)
    for cc in range(8):
        c0 = cc * 128
        wload(k, wpa[:, :, c0:c0 + 128], k.w_proj_a, c0, 128, 8, "wpa%d" % cc)
        wload(k, wpb[:, :, c0:c0 + 128], k.w_proj_b, c0, 128, 4, "wpb%d" % cc)
        wload(k, wga[:, :, c0:c0 + 128], k.w_in, OFF["ga"] + c0, 128, 8, "wga%d" % cc)
        wload(k, wgb[:, :, c0:c0 + 128], k.w_in, OFF["gb"] + c0, 128, 8, "wgb%d" % cc)
    wload(k, wo, k.w_out, 0, D, 8, "wo")
    junk3 = A.alloc(128, (D,), BF16)
    oa = [A.alloc(128, (8, 512), BF16) for _ in range(2)]
    obt = [A.alloc(128, (4, 512), BF16) for _ in range(2)]
    sga2 = [A.alloc(128, (512,), F32) for _ in range(2)]
    sgb2 = [A.alloc(128, (512,), F32) for _ in range(2)]
    m1 = A.alloc(128, (512,), F32)
    m2 = A.alloc(128, (512,), F32)
    mg = A.alloc(128, (8, 512), BF16)
    xt = [A.alloc(128, (D,), F32) for _ in range(2)]
    for b in range(8):
        t0 = b * 512
        s = b % 2
        hk = hTk[b * 4:(b + 1) * 4]
        DMA(P, "sp", oa[s], k.oaT_d[:, :, t0:t0 + 512].rearrange("h p t -> p h t"), reads=["oaT_d"], writes=["oa%d" % s])
        DMA(P, "sp", obt[s], k.obT_d[:, :, t0:t0 + 512].rearrange("h p t -> p h t"), reads=["obT_d"], writes=["obt%d" % s])
        for cc in range(8):
            cs = slice(cc * 128, (cc + 1) * 128)
            o_ = 4 * (cc % 2)
            pA, pB, pGa, pGb = psf[o_], psf[o_ + 1], psf[o_ + 2], psf[o_ + 3]
            kA, kB, kGa, kGb = ["psf%d" % o_], ["psf%d" % (o_ + 1)], ["psf%d" % (o_ + 2)], ["psf%d" % (o_ + 3)]
            for kc in range(8):
                MM(P, pGa[:, :], wga[:, kc, cs], hT[:, kc, t0:t0 + 512], start=(kc == 0), stop=(kc == 7), reads=["wga%d" % cc] + hk, excl=kGa)
            for kc in range(8):
                MM(P, pGb[:, :], wgb[:, kc, cs], hT[:, kc, t0:t0 + 512], start=(kc == 0), stop=(kc == 7), reads=["wgb%d" % cc] + hk, excl=kGb)
            for kc in range(8):
                MM(P, pA[:, :], wpa[:, kc, cs], oa[s][:, kc, :], start=(kc == 0), stop=(kc == 7), reads=["wpa%d" % cc, "oa%d" % s], excl=kA)
            for kc in range(4):
                MM(P, pB[:, :], wpb[:, kc, cs], obt[s][:, kc, :], start=(kc == 0), stop=(kc == 3), reads=["wpb%d" % cc, "obt%d" % s], excl=kB)
            sa, sb_ = sga2[cc % 2], sgb2[cc % 2]
            ACTV(P, sa, pGa[:, :], AF.Sigmoid, writes=["sga%d" % (cc % 2)], excl=kGa)
            ACTV(P, sb_, pGb[:, :], AF.Sigmoid, writes=["sgb%d" % (cc % 2)], excl=kGb)
            TT(P, "dve", m1, pA[:, :], sa, ALU.mult, reads=["sga%d" % (cc % 2)], writes=["m1"], excl=kA)
            TT(P, "dve", m2, pB[:, :], sb_, ALU.mult, reads=["sgb%d" % (cc % 2)], writes=["m2"], excl=kB)
            TT(P, "pool", mg[:, cc, :], m1, m2, ALU.add, reads=["m1", "m2"], writes=["mg%d" % cc])
        mgk = ["mg%d" % i for i in range(8)]
        for tt in range(4):
            tok = t0 + tt * 128
            xs = (b * 4 + tt) % 2
            xk = "x3t%d" % xs
            DMA(P, "sp", xt[xs], k.x[tok:tok + 128, :], writes=[xk])
            for half in range(2):
                pi = 2 * (tt % 2) + half
                hs_ = slice(half * 512, (half + 1) * 512)
                for cc in range(8):
                    MM(P, psf[pi][:, :], mg[:, cc, tt * 128:(tt + 1) * 128], wo[:, cc, hs_], start=(cc == 0), stop=(cc == 7),
                       reads=mgk + ["wo"], excl=["psf%d" % pi])
                TT(P, "dve", xt[xs][:, hs_], psf[pi][:, :], xt[xs][:, hs_], ALU.add, reads=[xk], writes=[xk], excl=["psf%d" % pi])
            DMA(P, "sp", k.x1_d[tok:tok + 128, :], xt[xs], reads=[xk], writes=["x1_d"])
            ti_ = b * 4 + tt
            ACTV(P, junk3, xt[xs], AF.Square, reads=[xk], writes=["junk3", "stat2_%d" % ti_], accum_out=k.stat2[:, ti_:ti_ + 1])
            if k.debug:
                DMA(P, "sp", k.dbg_x1[tok:tok + 128, :], xt[xs], reads=[xk])


def phase3b(k):
    P, A = k.P, k.A
    psf = k.psf
    st = k.stat
    ACTV(P, k.stat2[:, 32:64], k.stat2[:, 0:32], AF.Sqrt, writes=["rstd2a"], scale=1.0 / D, bias=EPS)
    RCP(P, k.stat2[:, 32:64], k.stat2[:, 32:64], reads=["rstd2a"], writes=["rstd2"])
    wd = A.alloc(128, (22, D), BF16)
    wload(k, wd, k.w_down, 0, D, 22, "wd")
    fw_bc = A.alloc(128, (D,), F32)
    DMA(P, "sp", fw_bc, bass.AP(k.final_norm_w.tensor, 0, [[0, 128], [1, D]]), writes=["fw_bc"])
    DMA(P, "sp", k.nw_bc[:], bass.AP(k.norm2_w.tensor, 0, [[0, 128], [1, D]]), writes=["nw2"])
    h2T = [A.alloc(128, (8, 1024), BF16) for _ in range(2)]
    actT = A.alloc(128, (22, 1024), BF16)
    wgu = [[A.alloc(128, (8, 512), BF16) for _ in range(2)] for _ in range(2)]
    sg = [A.alloc(128, (512,), F32) for _ in range(2)]
    x1t = [A.alloc(128, (D,), F32) for _ in range(2)]
    xt = [A.alloc(128, (D,), F32) for _ in range(2)]
    xn = [A.alloc(128, (D,), BF16) for _ in range(2)]
    junk = A.alloc(128, (D,), BF16)
    groups = [(0, 4), (4, 4), (8, 4), (12, 4), (16, 4), (20, 2)]
    ntc = [0]

    def norm_tile(sti, i):
        s_ = ntc[0] % 2
        ntc[0] += 1
        tile = sti * 8 + i
        tok = tile * 128
        dst = h2T[sti % 2][:, :, i * 128:(i + 1) * 128]
        DMA(P, "sp", xt[s_], k.x1_d[tok:tok + 128, :], reads=["x1_d"], writes=["nxt%d" % s_])
        STT(P, "dve", xn[s_], xt[s_], k.stat2[:, 32 + tile:33 + tile], k.nw_bc[:], ALU.mult, ALU.mult,
            reads=["nxt%d" % s_, "rstd2", "nw2"], writes=["nxn%d" % s_])
        pb = k.psb[s_]
        for kc in range(8):
            TRN(P, pb[:, kc * 128:(kc + 1) * 128], xn[s_][:, kc * 128:(kc + 1) * 128], k.ident_b[:],
                reads=["nxn%d" % s_, "ident_b"], excl=["psb%d" % s_])
        CPY(P, "act", dst, pb[:].rearrange("p (a b) -> p a b", a=8), writes=["h2T%d_%d" % (sti % 2, i)], excl=["psb%d" % s_])

    for i in range(8):
        norm_tile(0, i)
    gcount = 0
    for sti in range(4):
        tok0 = sti * 1024
        hb = sti % 2
        hkeys_ = ["h2T%d_%d" % (hb, i) for i in range(8)]
        nxt_tiles = list(range(8)) if sti + 1 < 4 else []
        for (f0, nf) in groups:
            ws = gcount % 2
            gcount += 1
            gsrc = k.w_gate_up[:, f0 * 128:(f0 + nf) * 128].rearrange("(kc p) c -> p kc c", p=128)
            usrc = k.w_gate_up[:, DFF + f0 * 128:DFF + (f0 + nf) * 128].rearrange("(kc p) c -> p kc c", p=128)
            DMA(P, "pool", wgu[ws][0][:, :, 0:nf * 128], gsrc, writes=["wgu%da" % ws])
            DMA(P, "pool", wgu[ws][1][:, :, 0:nf * 128], usrc, writes=["wgu%db" % ws])
            for fi in range(nf):
                fc = f0 + fi
                fsl = slice(fi * 128, (fi + 1) * 128)
                for tb in range(2):
                    pg, pu = (0, 1) if tb == 0 else (2, 3)
                    hk = hkeys_[tb * 4:(tb + 1) * 4]
                    tsl = slice(tb * 512, (tb + 1) * 512)
                    for kc in range(8):
                        MM(P, psf[pg][:, :], wgu[ws][0][:, kc, fsl], h2T[hb][:, kc, tsl], start=(kc == 0), stop=(kc == 7),
                           reads=["wgu%da" % ws] + hk, excl=["psf%d" % pg])
                    for kc in range(8):
                        MM(P, psf[pu][:, :], wgu[ws][1][:, kc, fsl], h2T[hb][:, kc, tsl], start=(kc == 0), stop=(kc == 7),
                           reads=["wgu%db" % ws] + hk, excl=["psf%d" % pu])
                    ACTV(P, sg[tb], psf[pg][:, :], AF.Silu, writes=["sg%d" % tb], excl=["psf%d" % pg])
                    TT(P, "dve", actT[:, fc, tsl], psf[pu][:, :], sg[tb], ALU.mult, reads=["sg%d" % tb],
                       writes=["actT%d_%d" % (fc, tb)], excl=["psf%d" % pu])
                if nxt_tiles and fc % 3 == 1 or (nxt_tiles and fc == 21):
                    norm_tile(sti + 1, nxt_tiles.pop(0))
        while nxt_tiles:
            norm_tile(sti + 1, nxt_tiles.pop(0))
        for tt in range(8):
            tok = tok0 + tt * 128
            xs = tt % 2
            xk = "x1t%d" % xs
            ak = ["actT%d_%d" % (fc, tt // 4) for fc in range(22)]
            DMA(P, "sp", x1t[xs], k.x1_d[tok:tok + 128, :], reads=["x1_d"], writes=[xk])
            for half in range(2):
                pi = 4 + half
                hs_ = slice(half * 512, (half + 1) * 512)
                for fc in range(22):
                    MM(P, psf[pi][:, :], actT[:, fc, tt * 128:(tt + 1) * 128], wd[:, fc, hs_], start=(fc == 0), stop=(fc == 21),
                       reads=ak + ["wd"], excl=["psf%d" % pi])
                TT(P, "dve", x1t[xs][:, hs_], psf[pi][:, :], x1t[xs][:, hs_], ALU.add, reads=[xk], writes=[xk], excl=["psf%d" % pi])
            sk = "fst%d" % xs
            ACTV(P, junk, x1t[xs], AF.Square, reads=[xk], writes=["fjunk", sk], accum_out=st[:, 48 + xs:49 + xs])
            ACTV(P, st[:, 50 + xs:51 + xs], st[:, 48 + xs:49 + xs], AF.Sqrt, reads=[sk], writes=[sk + "b"], scale=1.0 / D, bias=EPS)
            RCP(P, st[:, 52 + xs:53 + xs], st[:, 50 + xs:51 + xs], reads=[sk + "b"], writes=[sk + "c"])
            STT(P, "dve", x1t[xs], x1t[xs], st[:, 52 + xs:53 + xs], fw_bc, ALU.mult, ALU.mult,
                reads=[xk, sk + "c", "fw_bc"], writes=[xk])
            DMA(P, "sp", k.out[tok:tok + 128, :], x1t[xs], reads=[xk])


def make_consts():
    c = {}
    c["c_ident"] = np.eye(128, dtype=np.float32)
    kj = np.arange(128)[:, None]
    qi = np.arange(128)[None, :]
    mb = np.zeros((128, 256), np.float32)
    mb[:, 0:128] = np.where(kj <= qi, 1.0, 0.0)
    mb[:, 128:256] = np.where(kj >= qi, 1.0, 0.0)
    c["c_maskb"] = mb
    half = 16
    inv_freq = np.power(np.float32(500000.0), -np.arange(half, dtype=np.float32) * np.float32(2.0 / 32)).astype(np.float32)
    ang = (np.arange(T, dtype=np.float32)[:, None] * inv_freq[None, :]).astype(np.float32)
    cos = np.cos(ang).astype(np.float32).T
    sin = np.sin(ang).astype(np.float32).T
    c["c_cos"] = np.ascontiguousarray(np.concatenate([cos, cos], 0))
    c["c_sin"] = np.ascontiguousarray(np.concatenate([sin, sin], 0))
    pm = np.zeros((32, 32), np.float32)
    for i in range(16):
        pm[i + 16, i] = -1.0
        pm[i, i + 16] = 1.0
    c["c_pm"] = pm
    j = np.arange(CH)[:, None]
    cc = np.arange(CH)[None, :]
    c["c_tri"] = (j <= cc).astype(np.float32)
    c["c_u"] = (j > cc).astype(np.float32)
    return c


def make_in_maps(inputs, ncores=8):
    f = lambda a: np.ascontiguousarray(np.asarray(a, dtype=np.float32))
    shared = {
        "norm1_w": f(inputs["norm1_w"]).reshape(1, D),
        "w_in": f(inputs["w_in"])[0],
        "cwT": np.ascontiguousarray(f(inputs["conv_w"])[0].reshape(4, 3, 8, 128).transpose(3, 1, 2, 0).reshape(128, 96)),
        "a_log": f(inputs["a_log"]).reshape(1, 8),
        "dt_bias": f(inputs["dt_bias"]).reshape(1, 8),
        "dn_norm_w": f(inputs["dn_norm_w"]).reshape(128, 1),
        "w_proj_a": f(inputs["w_proj_a"])[0],
        "w_proj_b": f(inputs["w_proj_b"])[0],
        "w_out": f(inputs["w_out"])[0],
        "norm2_w": f(inputs["norm2_w"]).reshape(1, D),
        "w_gate_up": f(inputs["w_gate_up"])[0],
        "w_down": f(inputs["w_down"])[0],
        "final_norm_w": f(inputs["final_norm_w"]).reshape(1, D),
    }
    shared.update(make_consts())
    x = f(inputs["x"])
    maps = []
    for c in range(ncores):
        m = dict(shared)
        m["x"] = x[c]
        maps.append(m)
    return maps


_CACHE = {}


def kernel(**inputs):
    if "nc" not in _CACHE:
        _CACHE["nc"] = build_program()[0]
    nc = _CACHE["nc"]
    maps = make_in_maps(inputs, 8)
    res = run_bass_kernel_spmd(nc, maps, core_ids=list(range(8)))
    return np.stack([np.asarray(r["out"]) for r in res.results], axis=0).astype(np.float32)
```
